# Optimizing a Trainium2 kernel written in Bass

```python
import jax, jax.numpy as jnp
from jax import lax
import numpy as np

D_MODEL = 2048
BATCH = 8
SEQ = 2048
DEPTH = 2

GRID_W = 64
CTX_LEN = 256
HEAD_DIM = 128
N_Q_HEADS = D_MODEL // (2 * HEAD_DIM)
N_KV_HEADS = 2
GQA_GROUP = N_Q_HEADS // N_KV_HEADS
ATTN_WIDTH = N_Q_HEADS * HEAD_DIM
KV_WIDTH = N_KV_HEADS * HEAD_DIM
WINDOW = 128
BLOCK = 128
ROPE_BASE = 10000.0
ROT_AXIS_DIM = HEAD_DIM // 2
POOL_WINDOWS = (2, 4, 8, 16)
N_POOL_GROUPS = len(POOL_WINDOWS)
POOL_WIDTH = D_MODEL // 4
POOL_GROUP_DIM = POOL_WIDTH // N_POOL_GROUPS
SGU_WIDTH = D_MODEL // 4
N_SGU_HEADS = 4
SGU_HEAD_DIM = SGU_WIDTH // N_SGU_HEADS
SGU_CHUNK = 128
MIX_WIDTH = ATTN_WIDTH + POOL_WIDTH + SGU_WIDTH
Q_END = ATTN_WIDTH
K_END = Q_END + KV_WIDTH
V_END = K_END + KV_WIDTH
P_END = V_END + POOL_WIDTH
U_END = P_END + SGU_WIDTH
IN_WIDTH = U_END + SGU_WIDTH
D_FF = ((8 * D_MODEL // 3 + 255) // 256) * 256
CONV_WIDTH = 3
N_MOD = 6
EPS = 1e-6

kernel_name = "hybrid_parallel_heads_dit_block"


def rms_norm(x, g):
    xf = x.astype(jnp.float32)
    y = xf * lax.rsqrt(jnp.mean(xf * xf, axis=-1, keepdims=True) + EPS)
    return (y * g.astype(jnp.float32)).astype(x.dtype)


def modulate(h, shift, scale):
    return h * (1 + scale) + shift


def axial_rope_tables(L):
    rows = L // GRID_W
    row_ids = jnp.repeat(jnp.arange(rows), GRID_W).astype(jnp.float32)
    col_ids = jnp.tile(jnp.arange(GRID_W), rows).astype(jnp.float32)
    inv = 1.0 / (ROPE_BASE ** (jnp.arange(0, ROT_AXIS_DIM, 2, dtype=jnp.float32) / ROT_AXIS_DIM))
    ang_r = row_ids[:, None] * inv
    ang_c = col_ids[:, None] * inv
    return jnp.cos(ang_r), jnp.sin(ang_r), jnp.cos(ang_c), jnp.sin(ang_c)


def rotate_half_pairs(x, cos, sin):
    half = x.shape[-1] // 2
    x1, x2 = x[..., :half], x[..., half:]
    cos = cos[None, :, None, :]
    sin = sin[None, :, None, :]
    return jnp.concatenate([x1 * cos - x2 * sin, x2 * cos + x1 * sin], axis=-1)


def apply_axial_rope(x, tables):
    cr, sr, cc, sc = tables
    xf = x.astype(jnp.float32)
    yr = rotate_half_pairs(xf[..., :ROT_AXIS_DIM], cr, sr)
    yc = rotate_half_pairs(xf[..., ROT_AXIS_DIM:], cc, sc)
    return jnp.concatenate([yr, yc], axis=-1).astype(x.dtype)


def split_proj(p):
    return jnp.split(p, [Q_END, K_END, V_END, P_END, U_END], axis=-1)


def latent_window_attention(q, k, v, kc, vc, sink):
    B, L = q.shape[0], q.shape[1]
    nb = L // BLOCK
    scale = HEAD_DIM ** -0.5
    qb = q.reshape(B, nb, BLOCK, N_KV_HEADS, GQA_GROUP, HEAD_DIM)
    pad = ((0, 0), (1, 1), (0, 0), (0, 0), (0, 0))
    kp = jnp.pad(k.reshape(B, nb, BLOCK, N_KV_HEADS, HEAD_DIM), pad)
    vp = jnp.pad(v.reshape(B, nb, BLOCK, N_KV_HEADS, HEAD_DIM), pad)
    kband = jnp.concatenate([kp[:, :-2], kp[:, 1:-1], kp[:, 2:]], axis=2)
    vband = jnp.concatenate([vp[:, :-2], vp[:, 1:-1], vp[:, 2:]], axis=2)
    s_band = jnp.einsum('bnqkgd,bnskd->bnkgqs', qb, kband).astype(jnp.float32) * scale
    qi = jnp.arange(nb)[:, None] * BLOCK + jnp.arange(BLOCK)[None, :]
    kj = (jnp.arange(nb)[:, None] - 1) * BLOCK + jnp.arange(3 * BLOCK)[None, :]
    valid = ((kj[:, None, :] >= 0) & (kj[:, None, :] < L)
             & (jnp.abs(qi[:, :, None] - kj[:, None, :]) <= WINDOW))
    s_band = jnp.where(valid[None, :, None, None], s_band, -jnp.inf)
    s_ctx = jnp.einsum('bnqkgd,bskd->bnkgqs', qb, kc).astype(jnp.float32) * scale
    s_sink = jnp.broadcast_to(
        sink.astype(jnp.float32).reshape(N_KV_HEADS, GQA_GROUP)[None, None, :, :, None, None],
        s_band.shape[:-1] + (1,))
    p = jax.nn.softmax(jnp.concatenate([s_band, s_ctx, s_sink], axis=-1), axis=-1)
    nband = 3 * BLOCK
    nctx = kc.shape[1]
    p_band = p[..., :nband].astype(v.dtype)
    p_ctx = p[..., nband:nband + nctx].astype(v.dtype)
    o = (jnp.einsum('bnkgqs,bnskd->bnqkgd', p_band, vband)
         + jnp.einsum('bnkgqs,bskd->bnqkgd', p_ctx, vc))
    return o.reshape(B, L, ATTN_WIDTH)


def context_attention(qc, kc, vc, sink):
    B, Lc = qc.shape[0], qc.shape[1]
    qg = qc.reshape(B, Lc, N_KV_HEADS, GQA_GROUP, HEAD_DIM)
    s = jnp.einsum('bqkgd,bskd->bkgqs', qg, kc).astype(jnp.float32) * (HEAD_DIM ** -0.5)
    s_sink = jnp.broadcast_to(
        sink.astype(jnp.float32).reshape(N_KV_HEADS, GQA_GROUP)[None, :, :, None, None],
        s.shape[:-1] + (1,))
    p = jax.nn.softmax(jnp.concatenate([s, s_sink], axis=-1), axis=-1)[..., :-1]
    o = jnp.einsum('bkgqs,bskd->bqkgd', p.astype(vc.dtype), vc)
    return o.reshape(B, Lc, ATTN_WIDTH)


def multiscale_pool(pv, pool_w, pool_scale):
    B, L, _ = pv.shape
    xf = pv.astype(jnp.float32).reshape(B, L, N_POOL_GROUPS, POOL_GROUP_DIM)
    cs = jnp.pad(jnp.cumsum(xf, axis=1), ((0, 0), (1, 0), (0, 0), (0, 0)))
    t = jnp.arange(L)
    means = []
    for gi, w in enumerate(POOL_WINDOWS):
        lo = jnp.clip(t - w // 2, 0, L)
        hi = jnp.clip(t - w // 2 + w, 0, L)
        s = cs[:, hi, gi] - cs[:, lo, gi]
        means.append(s / (hi - lo).astype(jnp.float32)[None, :, None])
    pooled = jnp.stack(means, axis=2)
    y = (pooled - xf).astype(pv.dtype)
    y = jnp.einsum('blgc,gcd->blgd', y, pool_w)
    return y.reshape(B, L, POOL_WIDTH) * pool_scale


def spatial_gating(u, v, g_norm, w_s, b_s):
    B, L, _ = u.shape
    u = jax.nn.gelu(u, approximate=False)
    v = rms_norm(jax.nn.gelu(v, approximate=False), g_norm)
    vch = v.reshape(B, L // SGU_CHUNK, SGU_CHUNK, N_SGU_HEADS, SGU_HEAD_DIM)
    vs = jnp.einsum('hpq,bnqhc->bnphc', w_s, vch) + b_s.T[:, :, None]
    return u * vs.reshape(B, L, SGU_WIDTH)


def conv_ffn(h, w_up, conv_w, conv_b, w_down):
    L = h.shape[1]
    a = h @ w_up
    ap = jnp.pad(a, ((0, 0), (1, 1), (0, 0)))
    a = conv_b + ap[:, 0:L] * conv_w[0] + ap[:, 1:L + 1] * conv_w[1] + ap[:, 2:L + 2] * conv_w[2]
    gate, val = jnp.split(a, 2, axis=-1)
    return (jax.nn.silu(gate) * val) @ w_down


def setup_inputs(seed: int = 0) -> dict:
    key = jax.random.key(seed)
    ks = jax.random.split(key, 24)
    D = D_MODEL

    def nrm(k, shape, s):
        return jax.random.normal(k, shape, jnp.float32) * s

    return {
        "x": nrm(ks[0], (BATCH, SEQ, D), 1.0),
        "c": nrm(ks[1], (BATCH, D), 1.0),
        "ctx": nrm(ks[2], (BATCH, CTX_LEN, D), 1.0),
        "c_ctx": nrm(ks[3], (D,), 1.0),
        "norm1_g": 1.0 + nrm(ks[4], (DEPTH, D), 0.02),
        "norm2_g": 1.0 + nrm(ks[5], (DEPTH, D), 0.02),
        "w_ada": nrm(ks[6], (DEPTH, D, N_MOD * D), D ** -0.5),
        "b_ada": nrm(ks[7], (DEPTH, N_MOD * D), 0.02),
        "w_in": nrm(ks[8], (DEPTH, D, IN_WIDTH), D ** -0.5),
        "q_norm_g": 1.0 + nrm(ks[9], (DEPTH, HEAD_DIM), 0.02),
        "k_norm_g": 1.0 + nrm(ks[10], (DEPTH, HEAD_DIM), 0.02),
        "attn_sink": nrm(ks[11], (DEPTH, N_Q_HEADS), 0.5),
        "pool_w": nrm(ks[12], (DEPTH, N_POOL_GROUPS, POOL_GROUP_DIM, POOL_GROUP_DIM), POOL_GROUP_DIM ** -0.5),
        "pool_scale": 1.0 + nrm(ks[13], (DEPTH, POOL_WIDTH), 0.02),
        "sgu_norm_g": 1.0 + nrm(ks[14], (DEPTH, SGU_WIDTH), 0.02),
        "sgu_w": nrm(ks[15], (DEPTH, N_SGU_HEADS, SGU_CHUNK, SGU_CHUNK), SGU_CHUNK ** -0.5),
        "sgu_b": 1.0 + nrm(ks[16], (DEPTH, N_SGU_HEADS, SGU_CHUNK), 0.02),
        "w_out": nrm(ks[17], (DEPTH, MIX_WIDTH, D), MIX_WIDTH ** -0.5),
        "w_up": nrm(ks[18], (DEPTH, D, 2 * D_FF), D ** -0.5),
        "conv_w": nrm(ks[19], (DEPTH, CONV_WIDTH, 2 * D_FF), CONV_WIDTH ** -0.5),
        "conv_b": nrm(ks[20], (DEPTH, 2 * D_FF), 0.02),
        "w_down": nrm(ks[21], (DEPTH, D_FF, D), D_FF ** -0.5),
        "final_norm_g": 1.0 + nrm(ks[22], (D,), 0.02),
    }


def reference(x, c, ctx, c_ctx, norm1_g, norm2_g, w_ada, b_ada, w_in, q_norm_g, k_norm_g,
              attn_sink, pool_w, pool_scale, sgu_norm_g, sgu_w, sgu_b, w_out, w_up, conv_w,
              conv_b, w_down, final_norm_g):
    B, L, D = x.shape
    Lc = ctx.shape[1]
    rope = axial_rope_tables(L)
    silu_c = jax.nn.silu(c)
    silu_cc = jax.nn.silu(c_ctx)
    xc = ctx
    for l in range(DEPTH):
        last = l == DEPTH - 1
        mod = (silu_c @ w_ada[l] + b_ada[l]).reshape(B, N_MOD, 1, D)
        mod_c = (silu_cc @ w_ada[l] + b_ada[l]).reshape(N_MOD, D)

        h = modulate(rms_norm(x, norm1_g[l]), mod[:, 0], mod[:, 1])
        hc = modulate(rms_norm(xc, norm1_g[l]), mod_c[0], mod_c[1])
        q, k, v, pv, u, g = split_proj(h @ w_in[l])
        qc, kc, vc, pvc, uc, gc = split_proj(hc @ w_in[l])

        q = apply_axial_rope(rms_norm(q.reshape(B, L, N_Q_HEADS, HEAD_DIM), q_norm_g[l]), rope)
        k = apply_axial_rope(rms_norm(k.reshape(B, L, N_KV_HEADS, HEAD_DIM), k_norm_g[l]), rope)
        v = v.reshape(B, L, N_KV_HEADS, HEAD_DIM)
        kc = rms_norm(kc.reshape(B, Lc, N_KV_HEADS, HEAD_DIM), k_norm_g[l])
        vc = vc.reshape(B, Lc, N_KV_HEADS, HEAD_DIM)

        attn = latent_window_attention(q, k, v, kc, vc, attn_sink[l])
        pool = multiscale_pool(pv, pool_w[l], pool_scale[l])
        sgu = spatial_gating(u, g, sgu_norm_g[l], sgu_w[l], sgu_b[l])
        mix = jnp.concatenate([attn, pool, sgu], axis=-1) @ w_out[l]
        x = x + mod[:, 2] * mix

        if not last:
            qc = rms_norm(qc.reshape(B, Lc, N_Q_HEADS, HEAD_DIM), q_norm_g[l])
            attn_c = context_attention(qc, kc, vc, attn_sink[l])
            pool_c = multiscale_pool(pvc, pool_w[l], pool_scale[l])
            sgu_c = spatial_gating(uc, gc, sgu_norm_g[l], sgu_w[l], sgu_b[l])
            mix_c = jnp.concatenate([attn_c, pool_c, sgu_c], axis=-1) @ w_out[l]
            xc = xc + mod_c[2] * mix_c

        hf = modulate(rms_norm(x, norm2_g[l]), mod[:, 3], mod[:, 4])
        x = x + mod[:, 5] * conv_ffn(hf, w_up[l], conv_w[l], conv_b[l], w_down[l])
        if not last:
            hfc = modulate(rms_norm(xc, norm2_g[l]), mod_c[3], mod_c[4])
            xc = xc + mod_c[5] * conv_ffn(hfc, w_up[l], conv_w[l], conv_b[l], w_down[l])

    return rms_norm(x, final_norm_g)
```

```python
import contextlib
import math
import numpy as np
import concourse.bass as bass
import concourse.mybir as mybir
from concourse.bass_utils import run_bass_kernel_spmd

F32 = mybir.dt.float32
BF16 = mybir.dt.bfloat16
AF = mybir.ActivationFunctionType
ALU = mybir.AluOpType
AX = mybir.AxisListType

D = 2048
L = 2048
LC = 256
DEPTH = 2
DFF = 5632
NJ = 44
EPS = 1e-6
TM = 256
FFN_TILES = [(0, 410), (410, 410), (820, 410), (1230, 409), (1639, 409)]
ENGS = ("pe", "act", "dve", "pool", "sp")


class Res:
    __slots__ = ("name", "w", "r")

    def __init__(self, name):
        self.name = name
        self.w = None
        self.r = []


class Lane:
    __slots__ = ("name", "sem", "cnt", "step")

    def __init__(self, name, sem, step):
        self.name, self.sem, self.cnt, self.step = name, sem, 0, step


class FW:
    def __init__(self, nc, sems, n_dma_lanes):
        self.nc = nc
        self.ops = {e: [] for e in ENGS}
        self.lanes = {}
        it = iter(sems)
        for e in ("pe", "act", "dve", "pool"):
            self.lanes[e] = Lane(e, next(it), 1)
        self.dma_lanes = [Lane("dma%d" % i, next(it), 16) for i in range(n_dma_lanes)]
        half = n_dma_lanes // 2
        self.q_lanes = {"pool": self.dma_lanes[:half], "sp": self.dma_lanes[half:]}
        self.dma_rr = {"pool": 0, "sp": 0}
        self.seen = {e: {} for e in ENGS}

    def _deps(self, reads, writes):
        deps = {}

        def add(d):
            if d is None:
                return
            ln, idx = d
            if deps.get(ln, 0) < idx:
                deps[ln] = idx
        for r in reads:
            add(r.w)
        for w in writes:
            add(w.w)
            for d in w.r:
                add(d)
        return deps

    def _emit_waits(self, eng, deps, own_lane):
        for ln, idx in deps.items():
            if ln is own_lane and eng == "pe":
                continue
            if self.seen[eng].get(ln, 0) >= idx:
                continue
            self.seen[eng][ln] = idx
            self.ops[eng].append(lambda e, sem=ln.sem, val=idx * ln.step: e.wait_ge(sem, val))

    def _mark(self, lane, reads, writes):
        idx = lane.cnt
        for r in reads:
            r.r.append((lane, idx))
            if len(r.r) > 64:
                best = {}
                for ln, i in r.r:
                    if best.get(ln, 0) < i:
                        best[ln] = i
                r.r = list(best.items())
        for w in writes:
            w.w = (lane, idx)
            w.r = []

    def op(self, eng, fn, reads=(), writes=()):
        lane = self.lanes[eng]
        self._emit_waits(eng, self._deps(reads, writes), lane)
        lane.cnt += 1
        self.ops[eng].append(lambda e, fn=fn, sem=lane.sem: fn(e).then_inc(sem, 1))
        self._mark(lane, reads, writes)

    def dma(self, eng, fn, reads=(), writes=()):
        ql = self.q_lanes[eng]
        lane = ql[self.dma_rr[eng] % len(ql)]
        self.dma_rr[eng] += 1
        deps = self._deps(reads, writes)
        if lane.cnt > 0 and deps.get(lane, 0) < lane.cnt:
            deps[lane] = lane.cnt
        self._emit_waits(eng, deps, None)
        lane.cnt += 1
        self.ops[eng].append(lambda e, fn=fn, sem=lane.sem: fn(e).then_inc(sem, 16))
        self._mark(lane, reads, writes)

    def barrier(self):
        lanes = list(self.lanes.values()) + self.dma_lanes
        for eng in ENGS:
            for ln in lanes:
                if ln.cnt == 0 or self.seen[eng].get(ln, 0) >= ln.cnt:
                    continue
                if ln is self.lanes.get(eng) and eng == "pe":
                    continue
                self.seen[eng][ln] = ln.cnt
                self.ops[eng].append(lambda e, sem=ln.sem, val=ln.cnt * ln.step: e.wait_ge(sem, val))

    def wait_all_dma(self, eng):
        for ln in self.dma_lanes:
            if ln.cnt > 0:
                self.ops[eng].append(lambda e, sem=ln.sem, val=ln.cnt * 16: e.wait_ge(sem, val))

    def replay(self, block):
        m = {"pe": block.tensor, "act": block.scalar, "dve": block.vector,
             "pool": block.gpsimd, "sp": block.sync}
        for e in ENGS:
            ops = self.ops[e]
            if not ops:
                continue

            def body(engine, ops=ops):
                for f in ops:
                    f(engine)
            m[e](body)


def build_program(stop_after=None):
    nc = bass.Bass("TRN2", target_bir_lowering=False)

    def din(name, shape, dtype=F32):
        return nc.dram_tensor(name, list(shape), dtype, kind="ExternalInput").ap()

    xT_d = din("xT", [D, L])
    ctxT_d = din("ctxT", [D, LC])
    cc_d = din("cc", [128, 32])
    wada_d = din("wada", [DEPTH, 48, 128, 16, 256])
    bada_d = din("bada", [DEPTH, 128, 192])
    g1_d = din("g1", [DEPTH, 128, 16])
    g2_d = din("g2", [DEPTH, 128, 16])
    gf_d = din("gf", [128, 16])
    win_d = din("win", [DEPTH, 12, 128, 16, 256])
    wout_d = din("wout", [DEPTH, 8, 128, 16, 256])
    wup_d = din("wup", [DEPTH, NJ, 128, 16, 256])
    wdn_d = din("wdn", [DEPTH, 16, 2, 128, 22, 128])
    convw_d = din("convw", [DEPTH, 128, 352])
    qkg_d = din("qkg", [DEPTH, 128, 2])
    sink_d = din("sink", [DEPTH, 128, 8])
    poolw_d = din("poolw", [DEPTH, 128, 4, 128])
    poolsc_d = din("poolsc", [DEPTH, 128, 4])
    sgug_d = din("sgug", [DEPTH, 128, 512])
    sguw_d = din("sguw", [DEPTH, 128, 4, 128])
    sgub_d = din("sgub", [DEPTH, 128, 4, 128])
    cos_d = din("cosT", [128, L])
    sin_d = din("sinT", [128, L])
    rot_d = din("rotT", [128, 128])
    maskp_d = din("maskP", [128, 512])
    maskn_d = din("maskN", [128, 512])
    invl_d = din("invL", [128, 4, L])
    invc_d = din("invC", [128, 4, LC])
    out_d = nc.dram_tensor("outT", [D, L], F32, kind="ExternalOutput").ap()
    xm_d = nc.dram_tensor("xm", [D, L], F32, kind="Internal").ap()
    xa_d = nc.dram_tensor("xa", [D, L], F32, kind="Internal").ap()
    cm_d = nc.dram_tensor("cm", [D, LC], F32, kind="Internal").ap()
    ca_d = nc.dram_tensor("ca", [D, LC], F32, kind="Internal").ap()

    win_b = nc.dram_tensor("win_b", [DEPTH, 12, 128, 16, 256], BF16, kind="Internal").ap()
    wout_b = nc.dram_tensor("wout_b", [DEPTH, 8, 128, 16, 256], BF16, kind="Internal").ap()
    wup_b = nc.dram_tensor("wup_b", [DEPTH, NJ, 128, 16, 256], BF16, kind="Internal").ap()
    wdn_b = nc.dram_tensor("wdn_b", [DEPTH, 16, 2, 128, 22, 128], BF16, kind="Internal").ap()

    def fm(ap):
        return ap.rearrange("(c p) t -> p c t", p=128)

    es = contextlib.ExitStack()
    with es:
        N_DMA_LANES = 16
        sems = [es.enter_context(nc.semaphore("s%d" % i)) for i in range(4 + N_DMA_LANES)]

        def sb(name, shape, dtype=F32):
            return es.enter_context(nc.sbuf_tensor(name, list(shape), dtype))

        fw = FW(nc, sems, N_DMA_LANES)
        R = lambda n: Res(n)

        xbuf0 = sb("xbuf", [128, 16, 512]); hbuf0 = sb("hbuf", [128, 16, 512], BF16)
        r_xs = [[R("x%d_%d" % (i, c)) for c in range(16)] for i in range(2)]
        r_hs = [[R("h%d_%d" % (i, c)) for c in range(16)] for i in range(2)]
        NWR = 3
        wr = [sb("wr%d" % i, [128, 16, 256], BF16) for i in range(NWR)]; r_wr = [R("wr%d" % i) for i in range(NWR)]
        NTF = 4
        tf_ = [sb("tf%d" % i, [128, 512]) for i in range(NTF)]; r_tf = [R("tf%d" % i) for i in range(NTF)]
        NTB = 3
        tb_ = [sb("tb%d" % i, [128, 512], BF16) for i in range(NTB)]; r_tb = [R("tb%d" % i) for i in range(NTB)]
        kcT = sb("kcT", [128, 2, 256], BF16); r_kc = [R("kc0"), R("kc1")]
        vcS = sb("vcS", [128, 2, 256], BF16); r_vc = R("vc")
        ARENA_BYTES = 108288 + 1536
        arena = sb("arena", [128, ARENA_BYTES // 4])

        class Lay:
            def __init__(self):
                self.off = 0

            def get(self, shape, dtype=F32):
                n = 1
                for d_ in shape[1:]:
                    n *= d_
                nbytes = n * (4 if dtype == F32 else 2)
                assert self.off % 4 == 0 and nbytes % 4 == 0
                ap = arena[:, self.off // 4:(self.off + nbytes) // 4]
                if dtype != F32:
                    ap = ap.bitcast(dtype)
                if len(shape) == 3:
                    ap = ap.rearrange("p (a b) -> p a b", a=shape[1])
                elif len(shape) == 4:
                    ap = ap.rearrange("p (a b c) -> p a b c", a=shape[1], b=shape[2])
                self.off += nbytes
                assert self.off <= ARENA_BYTES, self.off
                return ap
        lm = Lay()
        xbuf1 = lm.get([128, 16, 512]); hbuf1 = lm.get([128, 16, 512], BF16)
        common_off = lm.off
        mixT = lm.get([128, 16, TM], BF16); r_mix = [R("mix%d" % c) for c in range(16)]
        qT = lm.get([128, 2, 2, 512], BF16); r_q = [[R("q%d%d" % (a, b)) for b in range(2)] for a in range(2)]
        kT = lm.get([128, 2, 512], BF16); r_k = [R("k0"), R("k1")]
        vE = lm.get([128, 4, 256], BF16); r_v = R("vE")
        WP = 544
        pvE = lm.get([128, 4, WP]); r_pv = [R("pv%d" % g) for g in range(4)]
        pA = lm.get([128, WP]); pB = lm.get([128, WP]); r_pA, r_pB = R("pA"), R("pB")
        ybf = [lm.get([128, TM], BF16) for i in range(4)]; r_y = [R("ybf%d" % i) for i in range(4)]
        ug = lm.get([128, 4, TM]); r_ug = [R("ug%d" % g) for g in range(4)]
        gtm = lm.get([128, 2, 512]); r_gtm = [R("gtm0"), R("gtm1")]
        vb = lm.get([128, 2, 512], BF16); r_vb = [R("vb0"), R("vb1")]
        cosS = lm.get([128, 512]); sinS = lm.get([128, 512]); r_rope = R("rope")
        NPT = 10
        pT = [lm.get([128, 512], BF16) for i in range(NPT)]; r_pT = [R("pT%d" % i) for i in range(NPT)]
        invS = lm.get([128, 4, TM]); r_inv = R("inv")
        rdn = lm.get([128, 128]); r_rdn = R("rdn")
        lf = Lay(); lf.off = common_off
        actT = lf.get([128, NJ, 412], BF16); r_act = [R("a%d" % j) for j in range(NJ)]
        NDN = 3
        dn = [lf.get([128, 22, 128], BF16) for i in range(NDN)]; r_dn = [R("dn%d" % i) for i in range(NDN)]
        xbufs = [xbuf0, xbuf1]; hbufs = [hbuf0, hbuf1]
        ssg = sb("ssg", [128, 2]); r_ssg = R("ssg")
        rsg = sb("rsg", [128, 2]); r_rsg = R("rsg")
        ones_bf = sb("ones_bf", [128, 128], BF16); r_ones = R("ones")
        rot_bf = sb("rot_bf", [128, 128], BF16); r_rot = R("rot")
        maskP = sb("maskPs", [128, 512], BF16); maskN = sb("maskNs", [128, 512], BF16); r_mask = R("mask")
        cc_sb = sb("cc_sb", [128, 32]); scc = sb("scc", [128, 32], BF16); r_cc = R("cc"); r_scc = R("scc")
        mods = [sb("mod%d" % i, [128, 192]) for i in range(DEPTH)]; r_mods = [R("mod%d" % i) for i in range(DEPTH)]
        bada = sb("bada_s", [128, 192]); r_bada = R("bada")
        g1s = sb("g1s", [128, 16]); g2s = sb("g2s", [128, 16]); gfs = sb("gfs", [128, 16]); r_g = R("g")
        A1s = [sb("A1_%d" % i, [128, 32]) for i in range(DEPTH)]; A2s = [sb("A2_%d" % i, [128, 32]) for i in range(DEPTH)]
        r_As = [R("A%d" % i) for i in range(DEPTH)]
        convw = sb("convw_s", [128, 352]); r_cw = R("cw")
        qkg = sb("qkg_s", [128, 2]); r_qkg = R("qkg")
        sinkS = sb("sinkS", [128, 8]); sinkE = sb("sinkE", [128, 8]); r_sink = R("sink")
        poolw_bf = sb("poolw_bf", [128, 4, 128], BF16); r_pw = R("poolw")
        poolsc = sb("poolsc_s", [128, 4]); r_psc = R("poolsc")
        sgug = sb("sgug_s", [128, 512]); r_sgug = R("sgug")
        sguw_bf = sb("sguw_bf", [128, 4, 128], BF16); r_sguw = R("sguw")
        sgub = sb("sgub_s", [128, 4, 128]); r_sgub = R("sgub")
        epsc = sb("epsc", [128, 4]); r_eps = R("eps")
        rsbuf = sb("rsbuf", [128, 512]); r_rsbuf = R("rsbuf")

        banks = [es.enter_context(nc.psum_tensor("ps%d" % i, [128, 512], F32)) for i in range(8)]
        r_bank = [R("bank%d" % i) for i in range(8)]
        st = {"bank": 0, "tf": 0, "tb": 0, "wr": 0, "dn": 0, "pT": 0}

        def nbank():
            i = st["bank"] % 7; st["bank"] += 1
            return banks[i], r_bank[i]

        def ntf():
            i = st["tf"] % NTF; st["tf"] += 1
            return tf_[i], r_tf[i]

        def ntb():
            i = st["tb"] % NTB; st["tb"] += 1
            return tb_[i], r_tb[i]

        def nwr():
            i = st["wr"] % NWR; st["wr"] += 1
            return wr[i], r_wr[i]

        def ndn():
            i = st["dn"] % NDN; st["dn"] += 1
            return dn[i], r_dn[i]

        def npt():
            i = st["pT"] % NPT; st["pT"] += 1
            return pT[i], r_pT[i]

        block = es.enter_context(nc.Block())

        def mmgroup(out_ap, pairs, reads, writes, first=True, last=True):
            def f(e):
                ins = None
                n = len(pairs)
                for i, (a, b) in enumerate(pairs):
                    ins = e.matmul(out_ap, lhsT=a, rhs=b, start=(first and i == 0), stop=(last and i == n - 1))
                return ins
            fw.op("pe", f, reads, writes)

        def load(dst_ap, src_ap, res, eng="sp", reads=()):
            fw.dma(eng, lambda e: e.dma_start(out=dst_ap, in_=src_ap), reads=reads, writes=res)

        def rsqrt_(out_ap, in_ap, eps_col, reads, wres):
            fw.op("act", lambda e: e.activation(out=out_ap, in_=in_ap, func=AF.Sqrt, bias=epsc[:, eps_col:eps_col + 1], scale=1.0),
                  reads=list(reads) + [r_eps], writes=[wres])
            fw.op("dve", lambda e: e.reciprocal(out=out_ap, in_=out_ap), reads=[wres], writes=[wres])

        r_win = [[R("win%d_%d" % (l, i)) for i in range(12)] for l in range(DEPTH)]
        r_wout = [[R("wout%d_%d" % (l, i)) for i in range(8)] for l in range(DEPTH)]
        r_wup = [[R("wup%d_%d" % (l, i)) for i in range(NJ)] for l in range(DEPTH)]
        r_wdn = [[[R("wdn%d_%d_%d" % (l, i, h)) for h in range(2)] for i in range(16)] for l in range(DEPTH)]
        cached = set()

        def wload(wt, rw, src_f32, scratch_bf, r_scr):
            key = id(r_scr)
            if key in cached:
                load(wt[:], scratch_bf, [rw], eng="pool", reads=[r_scr])
            else:
                load(wt[:], src_f32, [rw], eng="pool")
                load(scratch_bf, wt[:], [r_scr], eng="pool", reads=[rw])
                cached.add(key)

        fw.op("dve", lambda e: e.memset(epsc[:, 0:1], float(D) * EPS), writes=[r_eps])
        fw.op("dve", lambda e: e.memset(epsc[:, 1:2], 128.0 * EPS), writes=[r_eps])
        fw.op("dve", lambda e: e.memset(epsc[:, 2:3], 512.0 * EPS), writes=[r_eps])
        t0_, rt0 = ntf()
        fw.op("dve", lambda e: e.memset(t0_[:, 0:128], 1.0), writes=[rt0])
        fw.op("dve", lambda e: e.tensor_copy(out=ones_bf[:], in_=t0_[:, 0:128]), reads=[rt0], writes=[r_ones])
        load(rot_bf[:], rot_d, [r_rot], eng="pool")
        load(maskP[:], maskp_d, [r_mask], eng="pool")
        load(maskN[:], maskn_d, [r_mask], eng="pool")
        load(cc_sb[:], cc_d, [r_cc])
        fw.op("act", lambda e: e.activation(out=scc[:], in_=cc_sb[:], func=AF.Silu), reads=[r_cc], writes=[r_scc])
        load(gfs[:], gf_d, [r_g])

        SQD = math.sqrt(float(D))

        def layer_prologue(l):
            load(convw[:], convw_d[l], [r_cw])
            load(qkg[:], qkg_d[l], [r_qkg])
            load(sinkS[:], sink_d[l], [r_sink])
            load(poolsc[:], poolsc_d[l], [r_psc])
            load(sgug[:], sgug_d[l], [r_sgug])
            load(sgub[:], sgub_d[l], [r_sgub])
            load(poolw_bf[:], poolw_d[l], [r_pw], eng="pool")
            load(sguw_bf[:], sguw_d[l], [r_sguw], eng="pool")
            fw.op("act", lambda e: e.activation(out=sinkE[:], in_=sinkS[:], func=AF.Exp), reads=[r_sink], writes=[r_sink])
            fw.op("dve", lambda e: e.tensor_scalar(out=sgug[:], in0=sgug[:], scalar1=math.sqrt(512.0), scalar2=None, op0=ALU.mult),
                  reads=[r_sgug], writes=[r_sgug])

        def prologue_mod(l):
            mod, A1, A2, r_mod, r_A = mods[l], A1s[l], A2s[l], r_mods[l], r_As[l]
            load(bada[:], bada_d[l], [r_bada])
            load(g1s[:], g1_d[l], [r_g])
            load(g2s[:], g2_d[l], [r_g])
            fw.op("dve", lambda e: e.tensor_scalar(out=g1s[:], in0=g1s[:], scalar1=SQD, scalar2=None, op0=ALU.mult), reads=[r_g], writes=[r_g])
            fw.op("dve", lambda e: e.tensor_scalar(out=g2s[:], in0=g2s[:], scalar1=SQD, scalar2=None, op0=ALU.mult), reads=[r_g], writes=[r_g])
            bk, rb = banks[7], r_bank[7]
            for pp in range(48):
                wt, rw = nwr()
                load(wt[:], wada_d[l, pp], [rw], eng="pool")
                pairs = []
                for half in range(2):
                    n = 2 * pp + half
                    pairs.append((n, [(wt[:, k, half * 128:(half + 1) * 128], scc[:, 2 * k:2 * k + 2]) for k in range(16)]))

                def f(e, pairs=pairs):
                    ins = None
                    for n, pr in pairs:
                        for i, (a, b) in enumerate(pr):
                            ins = e.matmul(bk[:, 2 * n:2 * n + 2], lhsT=a, rhs=b, start=(i == 0), stop=(i == 15))
                    return ins
                fw.op("pe", f, reads=[rw, r_scc], writes=[rb])
                yield
            fw.op("dve", lambda e: e.tensor_tensor(out=mod[:], in0=bk[:, 0:192], in1=bada[:], op=ALU.add),
                  reads=[rb, r_bada], writes=[r_mod])
            for c in range(16):
                for m in range(2):
                    def fa(e, c=c, m=m):
                        return e.tensor_scalar(out=A1[:, 2 * c + m:2 * c + m + 1], in0=mod[:, (16 + c) * 2 + m:(16 + c) * 2 + m + 1],
                                               scalar1=1.0, scalar2=g1s[:, c:c + 1], op0=ALU.add, op1=ALU.mult)
                    fw.op("dve", fa, reads=[r_mod, r_g], writes=[r_A])

                    def fb(e, c=c, m=m):
                        return e.tensor_scalar(out=A2[:, 2 * c + m:2 * c + m + 1], in0=mod[:, (64 + c) * 2 + m:(64 + c) * 2 + m + 1],
                                               scalar1=1.0, scalar2=g2s[:, c:c + 1], op0=ALU.add, op1=ALU.mult)
                    fw.op("dve", fb, reads=[r_mod, r_g], writes=[r_A])

        hook = {"gen": None}

        def step_hook():
            g_ = hook["gen"]
            if g_ is not None:
                try:
                    next(g_)
                except StopIteration:
                    hook["gen"] = None

        def drain_hook():
            while hook["gen"] is not None:
                step_hook()

        def modcol(l, i, c, m):
            col = (i * 16 + c) * 2 + m
            return mods[l][:, col:col + 1]

        def norm_mod(l, s_, lo, hi, Acoef, shift_i, m):
            r_mod, r_A = r_mods[l], r_As[l]
            xbuf, hbuf, r_x, r_h = xbufs[s_], hbufs[s_], r_xs[s_], r_hs[s_]
            W = hi - lo
            for c in range(16):
                fw.op("act", lambda e, c=c: e.activation(out=hbuf[:, c, lo:hi], in_=xbuf[:, c, lo:hi], func=AF.Square),
                      reads=[r_x[c]], writes=[r_h[c]])
            bk, rb = nbank()
            mmgroup(bk[:, 0:W], [(ones_bf[:], hbuf[:, c, lo:hi]) for c in range(16)], reads=r_h + [r_ones], writes=[rb])
            rs, rrs = rsbuf, r_rsbuf
            rsqrt_(rs[:, 0:W], bk[:, 0:W], 0, [rb], rrs)
            for c in range(16):
                t, rt = ntf()
                fw.op("dve", lambda e, c=c, t=t: e.scalar_tensor_tensor(out=t[:, 0:W], in0=xbuf[:, c, lo:hi],
                                                                        scalar=Acoef[:, 2 * c + m:2 * c + m + 1], in1=rs[:, 0:W],
                                                                        op0=ALU.mult, op1=ALU.mult),
                      reads=[r_x[c], rrs, r_A], writes=[rt])
                fw.op("act", lambda e, c=c, t=t: e.activation(out=hbuf[:, c, lo:hi], in_=t[:, 0:W], func=AF.Identity,
                                                              bias=modcol(l, shift_i, c, m), scale=1.0),
                      reads=[rt, r_mod], writes=[r_h[c]])

        def head_norm_rope(bk, rb, W, gcol, rope, rope_c0, dests):
            sq, rsq = ntb()
            fw.op("act", lambda e: e.activation(out=sq[:, 0:W], in_=bk[:, 0:W], func=AF.Square), reads=[rb], writes=[rsq])
            b2, rb2 = nbank()
            mmgroup(b2[:, 0:W], [(ones_bf[:], sq[:, 0:W])], reads=[rsq, r_ones], writes=[rb2])
            rs, rrs = ntf()
            rsqrt_(rs[:, 0:W], b2[:, 0:W], 1, [rb2], rrs)
            if not rope:
                for (a, b, dap, dres) in dests:
                    fw.op("dve", lambda e, a=a, b=b, dap=dap: e.scalar_tensor_tensor(
                        out=dap, in0=bk[:, a:b], scalar=qkg[:, gcol:gcol + 1], in1=rs[:, a:b], op0=ALU.mult, op1=ALU.mult),
                        reads=[rb, rrs, r_qkg], writes=[dres])
                return
            qn, rqn = ntb()
            fw.op("dve", lambda e: e.scalar_tensor_tensor(out=qn[:, 0:W], in0=bk[:, 0:W], scalar=qkg[:, gcol:gcol + 1],
                                                          in1=rs[:, 0:W], op0=ALU.mult, op1=ALU.mult),
                  reads=[rb, rrs, r_qkg], writes=[rqn])

            def stage2():
                b3, rb3 = nbank()
                mmgroup(b3[:, 0:W], [(rot_bf[:], qn[:, 0:W])], reads=[rqn, r_rot], writes=[rb3])
                t1, rt1 = ntf()
                fw.op("dve", lambda e: e.tensor_tensor(out=t1[:, 0:W], in0=qn[:, 0:W], in1=cosS[:, rope_c0:rope_c0 + W], op=ALU.mult),
                      reads=[rqn, r_rope], writes=[rt1])
                t2, rt2 = ntf()
                fw.op("dve", lambda e: e.tensor_tensor(out=t2[:, 0:W], in0=b3[:, 0:W], in1=sinS[:, rope_c0:rope_c0 + W], op=ALU.mult),
                      reads=[rb3, r_rope], writes=[rt2])
                for (a, b, dap, dres) in dests:
                    fw.op("dve", lambda e, a=a, b=b, dap=dap: e.tensor_tensor(out=dap, in0=t1[:, a:b], in1=t2[:, a:b], op=ALU.add),
                          reads=[rt1, rt2], writes=[dres])
            return stage2

        def geo(t0, T, Lseq, m):
            lat = (m == 0)
            if lat:
                e0 = max(t0 - 128, 0); e1 = min(t0 + T + 128, Lseq)
            else:
                e0, e1 = 0, Lseq
            E = e1 - e0
            return lat, e0, e1, E, t0 - e0, T // 128, E // 128

        def mixer_prep(s_, l, src, dst, r_src, r_dst, t0, T, Lseq, m, full):
            xbuf, hbuf, r_x, r_h = xbufs[s_], hbufs[s_], r_xs[s_], r_hs[s_]
            lat, e0, e1, E, c0, nblk, nblk_e = geo(t0, T, Lseq, m)
            fw.dma("sp", lambda e: e.dma_start(out=xbuf[:, :, 0:E], in_=fm(src)[:, :, e0:e1]), reads=r_src, writes=r_x)
            norm_mod(l, s_, 0, E, A1s[l], 0, m)

        def mixer_body1(s_, l, src, dst, r_src, r_dst, t0, T, Lseq, m, full):
            xbuf, hbuf, r_x, r_h = xbufs[s_], hbufs[s_], r_xs[s_], r_hs[s_]
            lat, e0, e1, E, c0, nblk, nblk_e = geo(t0, T, Lseq, m)
            if lat:
                load(cosS[:, 0:E], cos_d[:, e0:e1], [r_rope])
                load(sinS[:, 0:E], sin_d[:, e0:e1], [r_rope])
            if full:
                load(invS[:, :, 0:T], (invl_d if lat else invc_d)[:, :, t0:t0 + T], [r_inv])
            if full:
                fw.op("dve", lambda e: e.memset(pvE[:], 0.0), writes=r_pv)
            kdst = kT if lat else kcT
            r_kdst = r_k if lat else r_kc
            vdst = vE if lat else vcS
            r_vdst = r_v if lat else r_vc
            pair_list = list(range(12)) if full else [4, 5]
            pend = []
            pend2 = []

            def defer(fn):
                while pend2:
                    pend2.pop(0)()
                if pend:
                    r_ = pend.pop(0)()
                    if r_ is not None:
                        pend2.append(r_)
                if fn is not None:
                    pend.append(fn)
            for pp in pair_list:
                wt, rw = nwr()
                wload(wt, rw, win_d[l, pp], win_b[l, pp], r_win[l][pp])
                for half in range(2):
                    j = 2 * pp + half
                    wsl = lambda k, half=half, wt=wt: wt[:, k, half * 128:(half + 1) * 128]
                    if j < 8 or 16 <= j < 20:
                        bk, rb = nbank()
                        mmgroup(bk[:, 0:T], [(wsl(k), hbuf[:, k, c0:c0 + T]) for k in range(16)], reads=r_h + [rw], writes=[rb])
                        if j < 8:
                            kh, g = j // 4, j % 4
                            dests = [(n * 128, (n + 1) * 128, qT[:, kh, n, g * 128:(g + 1) * 128], r_q[kh][n]) for n in range(nblk)]
                            defer(lambda bk=bk, rb=rb, dests=dests: head_norm_rope(bk, rb, T, 0, lat, c0, dests))
                        else:
                            gi = j - 16
                            defer(lambda gi=gi, bk=bk, rb=rb: fw.op(
                                "act", lambda e: e.activation(out=ug[:, gi, 0:T], in_=bk[:, 0:T], func=AF.Gelu),
                                reads=[rb], writes=[r_ug[gi]]))
                    elif 8 <= j < 10 or 12 <= j < 16:
                        bk, rb = nbank()
                        mmgroup(bk[:, 0:E], [(wsl(k), hbuf[:, k, 0:E]) for k in range(16)], reads=r_h + [rw], writes=[rb])
                        if j < 10:
                            kh = j - 8
                            defer(lambda bk=bk, rb=rb, kh=kh: head_norm_rope(bk, rb, E, 1, lat, 0, [(0, E, kdst[:, kh, 0:E], r_kdst[kh])]))
                        else:
                            gi = j - 12
                            defer(lambda gi=gi, bk=bk, rb=rb: fw.op(
                                "act", lambda e: e.activation(out=pvE[:, gi, 16:16 + E], in_=bk[:, 0:E], func=AF.Copy),
                                reads=[rb], writes=[r_pv[gi]]))
                    else:
                        isv = j < 12
                        nb_ = nblk_e if isv else nblk
                        cb = 0 if isv else c0
                        bk, rb = nbank()

                        def f(e, nb_=nb_, cb=cb, wsl=wsl, bk=bk):
                            ins = None
                            for b in range(nb_):
                                for k in range(16):
                                    ins = e.matmul(bk[:, b * 128:(b + 1) * 128], lhsT=hbuf[:, k, cb + b * 128:cb + (b + 1) * 128],
                                                   rhs=wsl(k), start=(k == 0), stop=(k == 15))
                            return ins
                        fw.op("pe", f, reads=r_h + [rw], writes=[rb])
                        def evac(nb_=nb_, isv=isv, j=j, bk=bk, rb=rb):
                            for b in range(nb_):
                                if isv:
                                    off = (j - 10) * 128
                                    fw.op("act", lambda e, b=b, off=off: e.activation(out=vdst[:, b, off:off + 128],
                                                                                      in_=bk[:, b * 128:(b + 1) * 128], func=AF.Copy),
                                          reads=[rb], writes=[r_vdst])
                                else:
                                    off = (j - 20) * 128
                                    fw.op("act", lambda e, b=b, off=off: e.activation(out=gtm[:, b, off:off + 128],
                                                                                      in_=bk[:, b * 128:(b + 1) * 128], func=AF.Gelu),
                                          reads=[rb], writes=[r_gtm[b]])
                        defer(evac)
            defer(None)
            defer(None)
            assert not pend and not pend2
            if not full:
                return
            SC = math.sqrt(128.0)
            def attn_front(kh, n):
                keys = []
                if lat:
                    nbg = t0 // 128 + n
                    be = c0 // 128 + n
                    if nbg > 0:
                        keys.append((kT[:, kh, (be - 1) * 128:be * 128], vE[:, be - 1, kh * 128:(kh + 1) * 128], maskP, [r_k[kh], r_v]))
                    keys.append((kT[:, kh, be * 128:(be + 1) * 128], vE[:, be, kh * 128:(kh + 1) * 128], None, [r_k[kh], r_v]))
                    if nbg < Lseq // 128 - 1:
                        keys.append((kT[:, kh, (be + 1) * 128:(be + 2) * 128], vE[:, be + 1, kh * 128:(kh + 1) * 128], maskN, [r_k[kh], r_v]))
                for b in range(2):
                    keys.append((kcT[:, kh, b * 128:(b + 1) * 128], vcS[:, b, kh * 128:(kh + 1) * 128], None, [r_kc[kh], r_vc]))
                ps = []
                for (kap, vap, msk, rr) in keys:
                    bs, rbs = nbank()
                    mmgroup(bs[:, :], [(kap, qT[:, kh, n, :])], reads=rr + [r_q[kh][n]], writes=[rbs])
                    p, rp = npt()
                    fw.op("act", lambda e, p=p, bs=bs: e.activation(out=p[:], in_=bs[:, :], func=AF.Exp, scale=SC),
                          reads=[rbs], writes=[rp])
                    if msk is not None:
                        fw.op("dve", lambda e, p=p, msk=msk: e.tensor_tensor(out=p[:], in0=p[:], in1=msk[:], op=ALU.mult),
                              reads=[rp, r_mask], writes=[rp])
                    ps.append((p, rp, vap, rr))
                return ps

            def attn_back(kh, n, ps):
                bo, rbo = nbank()
                bd, rbd = nbank()
                nk = len(ps)
                for i, (p, rp, vap, rr) in enumerate(ps):
                    mmgroup(bo[:, :], [(vap, p[:])], reads=rr + [rp], writes=[rbo], first=(i == 0), last=(i == nk - 1))
                    mmgroup(bd[:, :], [(ones_bf[:], p[:])], reads=[rp, r_ones], writes=[rbd], first=(i == 0), last=(i == nk - 1))
                rd, rrd = ntf()
                for g in range(4):
                    hq = kh * 4 + g
                    fw.op("act", lambda e, g=g, hq=hq: e.activation(out=rd[:, g * 128:(g + 1) * 128], in_=bd[:, g * 128:(g + 1) * 128],
                                                                     func=AF.Identity, bias=sinkE[:, hq:hq + 1], scale=1.0),
                          reads=[rbd, r_sink], writes=[rrd])
                fw.op("dve", lambda e: e.reciprocal(out=rd[:], in_=rd[:]), reads=[rrd], writes=[rrd])
                fw.op("dve", lambda e: e.tensor_tensor(out=mixT[:, 4 * kh:4 * kh + 4, n * 128:(n + 1) * 128],
                                                       in0=bo[:, :].rearrange("p (g q) -> p g q", g=4),
                                                       in1=rd[:].rearrange("p (g q) -> p g q", g=4), op=ALU.mult),
                      reads=[rbo, rrd], writes=r_mix[4 * kh:4 * kh + 4])
            pc0 = 16 + c0

            def pool_dve(gi):
                cur = None
                bufs = [(pA, r_pA), (pB, r_pB)]
                for lv in range(1, gi + 2):
                    hw = 1 << (lv - 1)
                    if lv == 1:
                        lo_, hi_ = 1, WP - 1
                        o, ro = bufs[0]
                        fw.op("dve", lambda e, o=o, lo_=lo_, hi_=hi_: e.tensor_tensor(
                            out=o[:, lo_:hi_], in0=pvE[:, gi, lo_ - 1:hi_ - 1], in1=pvE[:, gi, lo_:hi_], op=ALU.add),
                            reads=[r_pv[gi]], writes=[ro])
                        cur = (o, ro)
                    else:
                        sh = hw // 2
                        lo_, hi_ = hw, WP - hw
                        o, ro = bufs[(lv - 1) % 2]
                        ci, rci = cur
                        fw.op("dve", lambda e, o=o, ci=ci, lo_=lo_, hi_=hi_, sh=sh: e.tensor_tensor(
                            out=o[:, lo_:hi_], in0=ci[:, lo_ - sh:hi_ - sh], in1=ci[:, lo_ + sh:hi_ + sh], op=ALU.add),
                            reads=[rci], writes=[ro])
                        cur = (o, ro)
                s2, rs_ = cur
                t, rt = ntf()
                fw.op("dve", lambda e: e.tensor_tensor(out=t[:, 0:T], in0=s2[:, pc0:pc0 + T], in1=invS[:, gi, 0:T], op=ALU.mult),
                      reads=[rs_, r_inv], writes=[rt])
                fw.op("dve", lambda e: e.tensor_tensor(out=ybf[gi][:, 0:T], in0=t[:, 0:T], in1=pvE[:, gi, pc0:pc0 + T], op=ALU.subtract),
                      reads=[rt, r_pv[gi]], writes=[r_y[gi]])

            def pool_pe(gi):
                bk, rb = nbank()
                mmgroup(bk[:, 0:T], [(poolw_bf[:, gi, :], ybf[gi][:, 0:T])], reads=[r_pw, r_y[gi]], writes=[rb])
                fw.op("act", lambda e: e.activation(out=mixT[:, 8 + gi, 0:T], in_=bk[:, 0:T], func=AF.Identity,
                                                    scale=poolsc[:, gi:gi + 1]),
                      reads=[rb, r_psc], writes=[r_mix[8 + gi]])

            def sgu_pre():
                for b in range(nblk):
                    t, rt = ntf()
                    fw.op("act", lambda e, t=t, b=b: e.activation(out=t[:], in_=gtm[:, b, :], func=AF.Square), reads=[r_gtm[b]], writes=[rt])
                    fw.op("dve", lambda e, t=t, b=b: e.tensor_reduce(out=ssg[:, b:b + 1], in_=t[:], axis=AX.X, op=ALU.add),
                          reads=[rt], writes=[r_ssg])
                rsqrt_(rsg[:, 0:nblk], ssg[:, 0:nblk], 2, [r_ssg], r_rsg)
                for b in range(nblk):
                    fw.op("dve", lambda e, b=b: e.scalar_tensor_tensor(out=vb[:, b, :], in0=gtm[:, b, :], scalar=rsg[:, b:b + 1],
                                                                       in1=sgug[:], op0=ALU.mult, op1=ALU.mult),
                          reads=[r_gtm[b], r_rsg, r_sgug], writes=[r_vb[b]])

            def sgu_pe(h):
                bk, rb = nbank()

                def f(e):
                    ins = None
                    for b in range(nblk):
                        ins = e.matmul(bk[:, b * 128:(b + 1) * 128], lhsT=vb[:, b, h * 128:(h + 1) * 128], rhs=sguw_bf[:, h, :],
                                       start=True, stop=True)
                    return ins
                fw.op("pe", f, reads=r_vb[:nblk] + [r_sguw], writes=[rb])
                t, rt = ntf()
                for b in range(nblk):
                    fw.op("dve", lambda e, b=b: e.tensor_tensor(out=t[:, b * 128:(b + 1) * 128], in0=bk[:, b * 128:(b + 1) * 128],
                                                                in1=sgub[:, h, :], op=ALU.add),
                          reads=[rb, r_sgub], writes=[rt])
                fw.op("dve", lambda e: e.tensor_tensor(out=mixT[:, 12 + h, 0:T], in0=t[:, 0:T], in1=ug[:, h, 0:T], op=ALU.mult),
                      reads=[rt, r_ug[h]], writes=[r_mix[12 + h]])

            its = [(kh, n) for kh in range(2) for n in range(nblk)]
            sgu_pre()
            front = attn_front(*its[0])
            for ii, (kh, n) in enumerate(its):
                nxt = attn_front(*its[ii + 1]) if ii + 1 < len(its) else None
                if ii < 4:
                    pool_dve(ii)
                attn_back(kh, n, front)
                if ii < 4:
                    pool_pe(ii)
                    sgu_pe(ii)
                front = nxt
            for ii in range(len(its), 4):
                pool_dve(ii); pool_pe(ii); sgu_pe(ii)

        def mixer_body2(s_, l, src, dst, r_src, r_dst, t0, T, Lseq, m, full):
            xbuf, hbuf, r_x, r_h = xbufs[s_], hbufs[s_], r_xs[s_], r_hs[s_]
            lat, e0, e1, E, c0, nblk, nblk_e = geo(t0, T, Lseq, m)
            if not full:
                return
            import os
            _br = os.environ.get("DBG_BR", "aps")
            for ch, lo_c, hi_c in (("a", 0, 8), ("p", 8, 12), ("s", 12, 16)):
                if ch not in _br:
                    for cc_ in range(lo_c, hi_c):
                        fw.op("dve", lambda e, cc_=cc_: e.memset(mixT[:, cc_, 0:T], 0.0), writes=[r_mix[cc_]])
            for pp in range(8):
                wt, rw = nwr()
                wload(wt, rw, wout_d[l, pp], wout_b[l, pp], r_wout[l][pp])
                for half in range(2):
                    n = 2 * pp + half
                    bk, rb = nbank()
                    mmgroup(bk[:, 0:T], [(wt[:, k, half * 128:(half + 1) * 128], mixT[:, k, 0:T]) for k in range(16)],
                            reads=r_mix + [rw], writes=[rb])
                    fw.op("dve", lambda e, n=n, bk=bk: e.scalar_tensor_tensor(out=xbuf[:, n, c0:c0 + T], in0=bk[:, 0:T], scalar=modcol(l, 2, n, m),
                                                                              in1=xbuf[:, n, c0:c0 + T], op0=ALU.mult, op1=ALU.add),
                          reads=[rb, r_mods[l], r_x[n]], writes=[r_x[n]])
            fw.dma("sp", lambda e: e.dma_start(out=fm(dst)[:, :, t0:t0 + T], in_=xbuf[:, :, c0:c0 + T]), reads=r_x, writes=r_dst)

        def ffn_prep(s_, l, src, dst, r_src, r_dst, t0, T, Lseq, m, final):
            xbuf, hbuf, r_x, r_h = xbufs[s_], hbufs[s_], r_xs[s_], r_hs[s_]
            E = T + 2
            lo = 1 if t0 == 0 else 0
            hi = E - 1 if t0 + T == Lseq else E
            e0 = t0 - 1
            fw.dma("sp", lambda e: e.dma_start(out=xbuf[:, :, lo:hi], in_=fm(src)[:, :, e0 + lo:e0 + hi]), reads=r_src, writes=r_x)
            norm_mod(l, s_, lo, hi, A2s[l], 3, m)
            if lo == 1:
                fw.op("dve", lambda e: e.memset(hbuf[:, :, 0:1], 0.0), writes=r_h)
            if hi == E - 1:
                fw.op("dve", lambda e: e.memset(hbuf[:, :, E - 1:E], 0.0), writes=r_h)

        def ffn_body1(s_, l, src, dst, r_src, r_dst, t0, T, Lseq, m, final):
            xbuf, hbuf, r_x, r_h = xbufs[s_], hbufs[s_], r_xs[s_], r_hs[s_]
            E = T + 2
            for j in range(NJ):
                wt, rw = nwr()
                wload(wt, rw, wup_d[l, j], wup_b[l, j], r_wup[l][j])
                bg, rbg = nbank()
                mmgroup(bg[:, 0:E], [(wt[:, k, 0:128], hbuf[:, k, 0:E]) for k in range(16)], reads=r_h + [rw], writes=[rbg])
                bv, rbv = nbank()
                mmgroup(bv[:, 0:E], [(wt[:, k, 128:256], hbuf[:, k, 0:E]) for k in range(16)], reads=r_h + [rw], writes=[rbv])
                step_hook()
                outs = []
                for (bk, rb, jj) in ((bg, rbg, j), (bv, rbv, NJ + j)):
                    cw = lambda tap, jj=jj: convw[:, jj * 4 + tap:jj * 4 + tap + 1]
                    ta, rta = ntf()
                    fw.op("act", lambda e, bk=bk, ta=ta, cw=cw: e.activation(out=ta[:, 0:T], in_=bk[:, 1:T + 1], func=AF.Identity,
                                                                             bias=cw(3), scale=cw(1)),
                          reads=[rb, r_cw], writes=[rta])
                    fw.op("dve", lambda e, bk=bk, ta=ta, cw=cw: e.scalar_tensor_tensor(out=ta[:, 0:T], in0=bk[:, 0:T], scalar=cw(0),
                                                                                       in1=ta[:, 0:T], op0=ALU.mult, op1=ALU.add),
                          reads=[rb, r_cw, rta], writes=[rta])
                    fw.op("dve", lambda e, bk=bk, ta=ta, cw=cw: e.scalar_tensor_tensor(out=ta[:, 0:T], in0=bk[:, 2:T + 2], scalar=cw(2),
                                                                                       in1=ta[:, 0:T], op0=ALU.mult, op1=ALU.add),
                          reads=[rb, r_cw, rta], writes=[rta])
                    outs.append((ta, rta))
                (tg, rtg), (tv, rtv) = outs
                sg, rsg_ = ntf()
                fw.op("act", lambda e, tg=tg, sg=sg: e.activation(out=sg[:, 0:T], in_=tg[:, 0:T], func=AF.Silu), reads=[rtg], writes=[rsg_])
                fw.op("dve", lambda e, sg=sg, tv=tv, j=j: e.tensor_tensor(out=actT[:, j, 0:T], in0=sg[:, 0:T], in1=tv[:, 0:T], op=ALU.mult),
                      reads=[rsg_, rtv], writes=[r_act[j]])
        def ffn_body2(s_, l, src, dst, r_src, r_dst, t0, T, Lseq, m, final):
            xbuf, hbuf, r_x, r_h = xbufs[s_], hbufs[s_], r_xs[s_], r_hs[s_]
            for n in range(16):
                bk, rb = nbank()
                for hf in range(2):
                    wd, rwd = ndn()
                    wload(wd, rwd, wdn_d[l, n, hf], wdn_b[l, n, hf], r_wdn[l][n][hf])
                    mmgroup(bk[:, 0:T], [(wd[:, jj, :], actT[:, hf * 22 + jj, 0:T]) for jj in range(22)],
                            reads=r_act[hf * 22:(hf + 1) * 22] + [rwd], writes=[rb], first=(hf == 0), last=(hf == 1))
                fw.op("dve", lambda e, n=n, bk=bk: e.scalar_tensor_tensor(out=xbuf[:, n, 1:T + 1], in0=bk[:, 0:T], scalar=modcol(l, 5, n, m),
                                                                          in1=xbuf[:, n, 1:T + 1], op0=ALU.mult, op1=ALU.add),
                      reads=[rb, r_mods[l], r_x[n]], writes=[r_x[n]])
            if final:
                for c in range(16):
                    fw.op("act", lambda e, c=c: e.activation(out=hbuf[:, c, 1:T + 1], in_=xbuf[:, c, 1:T + 1], func=AF.Square),
                          reads=[r_x[c]], writes=[r_h[c]])
                bk, rb = nbank()
                mmgroup(bk[:, 0:T], [(ones_bf[:], hbuf[:, c, 1:T + 1]) for c in range(16)], reads=r_h + [r_ones], writes=[rb])
                rs, rrs = rsbuf, r_rsbuf
                rsqrt_(rs[:, 0:T], bk[:, 0:T], 0, [rb], rrs)
                for c in range(16):
                    fw.op("dve", lambda e, c=c: e.scalar_tensor_tensor(out=xbuf[:, c, 1:T + 1], in0=xbuf[:, c, 1:T + 1],
                                                                       scalar=gfs[:, c:c + 1], in1=rs[:, 0:T], op0=ALU.mult, op1=ALU.mult),
                          reads=[r_x[c], rrs, r_g], writes=[r_x[c]])
            fw.dma("sp", lambda e: e.dma_start(out=fm(dst)[:, :, t0:t0 + T], in_=xbuf[:, :, 1:T + 1]), reads=r_x, writes=r_dst)

        fw.op("dve", lambda e: e.tensor_scalar(out=gfs[:], in0=gfs[:], scalar1=SQD, scalar2=None, op0=ALU.mult), reads=[r_g], writes=[r_g])
        r_xT = [R("dxT")]; r_ctxT = [R("dctx")]
        r_xm = [R("dxm")]; r_xa = [R("dxa")]; r_cm = [R("dcm")]; r_ca = [R("dca")]; r_out = [R("dout")]
        lat_src = [(xT_d, r_xT), (xa_d, r_xa)]
        ctx_src = [(ctxT_d, r_ctxT), (ca_d, r_ca)]
        done = False
        def run_phase(tiles, prep, body1, body2):
            if not tiles:
                return
            prep(0, *tiles[0])
            for i, t in enumerate(tiles):
                s_ = i % 2
                body1(s_, *t)
                if i + 1 < len(tiles):
                    prep(1 - s_, *tiles[i + 1])
                body2(s_, *t)

        for l in range(DEPTH):
            last = (l == DEPTH - 1)
            if l == 0:
                hook["gen"] = prologue_mod(0)
            drain_hook()
            layer_prologue(l)
            cs, rcs = ctx_src[l]
            xs, rxs = lat_src[l]
            mtiles = [(l, cs, cm_d, rcs, r_cm, 0, LC, LC, 1, not last)]
            mtiles += [(l, xs, xm_d, rxs, r_xm, i * TM, TM, L, 0, True) for i in range(L // TM)]
            fw.barrier()
            run_phase(mtiles, mixer_prep, mixer_body1, mixer_body2)
            if stop_after == "mix%d" % l:
                done = True
                break
            dst, rdst = (out_d, r_out) if last else (xa_d, r_xa)
            ftiles = [(l, xm_d, dst, r_xm, rdst, t0, T, L, 0, last) for (t0, T) in FFN_TILES]
            if not last:
                ftiles.insert(1, (l, cm_d, ca_d, r_cm, r_ca, 0, LC, LC, 1, False))
            fw.barrier()
            if not last:
                hook["gen"] = prologue_mod(l + 1)
            run_phase(ftiles, ffn_prep, ffn_body1, ffn_body2)
            if stop_after == "ffn%d" % l:
                done = True
                break
        fw.barrier()
        xbuf, r_x = xbuf0, r_xs[0]
        if stop_after is not None and stop_after.startswith("mix"):
            for i in range(4):
                fw.dma("sp", lambda e, i=i: e.dma_start(out=xbuf[:, :, :], in_=fm(xm_d)[:, :, i * 512:(i + 1) * 512]), reads=r_xm, writes=r_x)
                fw.dma("sp", lambda e, i=i: e.dma_start(out=fm(out_d)[:, :, i * 512:(i + 1) * 512], in_=xbuf[:, :, :]), reads=r_x, writes=r_out)
        elif stop_after is not None and stop_after == "ffn0":
            for i in range(4):
                fw.dma("sp", lambda e, i=i: e.dma_start(out=xbuf[:, :, :], in_=fm(xa_d)[:, :, i * 512:(i + 1) * 512]), reads=r_xa, writes=r_x)
                fw.dma("sp", lambda e, i=i: e.dma_start(out=fm(out_d)[:, :, i * 512:(i + 1) * 512], in_=xbuf[:, :, :]), reads=r_x, writes=r_out)
        fw.wait_all_dma("sp")
        fw.replay(block)
    return nc


def _tile_kn(w, nw):
    K, N = w.shape
    return np.ascontiguousarray(w.reshape(K // 128, 128, N // nw, nw).transpose(2, 1, 0, 3))


def _fmvec(v):
    return np.ascontiguousarray(v.reshape(-1, 128).T)


def _const_tables():
    inv = (1.0 / (10000.0 ** (np.arange(0, 64, 2, dtype=np.float32) / np.float32(64)))).astype(np.float32)
    t = np.arange(L)
    row = (t // 64).astype(np.float32)
    col = (t % 64).astype(np.float32)
    ang = np.zeros((128, L), np.float32)
    for d in range(128):
        pos = row if d < 64 else col
        ang[d] = pos * inv[d % 32]
    cosT = np.cos(ang).astype(np.float32)
    sinT = np.sin(ang).astype(np.float32)
    rotT = np.zeros((128, 128), np.float32)
    for dp in range(128):
        if (dp % 64) < 32:
            rotT[dp + 32, dp] = -1.0
        else:
            rotT[dp - 32, dp] = 1.0
    s = np.arange(128)[:, None]
    q = np.arange(128)[None, :]
    mp = (s >= q).astype(np.float32)
    mn = (s <= q).astype(np.float32)
    maskP = np.tile(mp, (1, 4))
    maskN = np.tile(mn, (1, 4))

    def invtab(Ls):
        tt = np.arange(Ls)
        out = np.zeros((4, Ls), np.float32)
        for gi, w in enumerate((2, 4, 8, 16)):
            lo = np.clip(tt - w // 2, 0, Ls)
            hi = np.clip(tt - w // 2 + w, 0, Ls)
            out[gi] = 1.0 / (hi - lo).astype(np.float32)
        return np.ascontiguousarray(np.broadcast_to(out[None], (128, 4, Ls)))
    return dict(cosT=cosT, sinT=sinT, rotT=rotT, maskP=maskP, maskN=maskN, invL=invtab(L), invC=invtab(LC))


def _prep_shared(inp):
    f = lambda a: np.asarray(a, dtype=np.float32)
    sh = {}
    w_ada = f(inp["w_ada"])
    sh["wada"] = np.stack([_tile_kn(w_ada[l], 256) for l in range(DEPTH)])
    b_ada = f(inp["b_ada"])
    sh["bada"] = np.stack([np.repeat(_fmvec(b_ada[l]), 2, axis=1) for l in range(DEPTH)])
    sh["g1"] = np.stack([_fmvec(f(inp["norm1_g"])[l]) for l in range(DEPTH)])
    sh["g2"] = np.stack([_fmvec(f(inp["norm2_g"])[l]) for l in range(DEPTH)])
    sh["gf"] = _fmvec(f(inp["final_norm_g"]))
    w_in = f(inp["w_in"])
    sh["win"] = np.stack([_tile_kn(w_in[l], 256) for l in range(DEPTH)])
    w_out = f(inp["w_out"])
    sh["wout"] = np.stack([_tile_kn(w_out[l], 256) for l in range(DEPTH)])
    w_up = f(inp["w_up"])
    ups = []
    for l in range(DEPTH):
        gt = _tile_kn(w_up[l][:, :DFF], 128)
        vt = _tile_kn(w_up[l][:, DFF:], 128)
        ups.append(np.concatenate([gt, vt], axis=3))
    sh["wup"] = np.stack(ups)
    w_dn = f(inp["w_down"])
    dns = []
    for l in range(DEPTH):
        a = w_dn[l].reshape(2, 22, 128, 16, 128).transpose(3, 0, 2, 1, 4)
        dns.append(np.ascontiguousarray(a))
    sh["wdn"] = np.stack(dns)
    cw = f(inp["conv_w"]); cb = f(inp["conv_b"])
    cws = []
    for l in range(DEPTH):
        a = np.concatenate([cw[l], cb[l][None]], axis=0)
        a = a.reshape(4, 88, 128).transpose(2, 1, 0).reshape(128, 352)
        cws.append(np.ascontiguousarray(a))
    sh["convw"] = np.stack(cws)
    sh["qkg"] = np.stack([np.stack([f(inp["q_norm_g"])[l], f(inp["k_norm_g"])[l]], axis=1) for l in range(DEPTH)])
    sh["sink"] = np.stack([np.ascontiguousarray(np.broadcast_to(f(inp["attn_sink"])[l][None], (128, 8))) for l in range(DEPTH)])
    sh["poolw"] = np.stack([np.ascontiguousarray(f(inp["pool_w"])[l].transpose(1, 0, 2)) for l in range(DEPTH)])
    sh["poolsc"] = np.stack([_fmvec(f(inp["pool_scale"])[l]) for l in range(DEPTH)])
    sh["sgug"] = np.stack([np.ascontiguousarray(np.broadcast_to(f(inp["sgu_norm_g"])[l][None], (128, 512))) for l in range(DEPTH)])
    sh["sguw"] = np.stack([np.ascontiguousarray(f(inp["sgu_w"])[l].transpose(2, 0, 1)) for l in range(DEPTH)])
    sh["sgub"] = np.stack([np.ascontiguousarray(np.broadcast_to(f(inp["sgu_b"])[l][None], (128, 4, 128))) for l in range(DEPTH)])
    sh.update(_const_tables())
    return sh


_NC_CACHE = {}


def kernel(x, c, ctx, c_ctx, norm1_g, norm2_g, w_ada, b_ada, w_in, q_norm_g, k_norm_g,
           attn_sink, pool_w, pool_scale, sgu_norm_g, sgu_w, sgu_b, w_out, w_up, conv_w,
           conv_b, w_down, final_norm_g, _stop_after=None, _cores=None, _trace=False):
    inp = dict(w_ada=w_ada, b_ada=b_ada, norm1_g=norm1_g, norm2_g=norm2_g, final_norm_g=final_norm_g,
               w_in=w_in, w_out=w_out, w_up=w_up, w_down=w_down, conv_w=conv_w, conv_b=conv_b,
               q_norm_g=q_norm_g, k_norm_g=k_norm_g, attn_sink=attn_sink, pool_w=pool_w,
               pool_scale=pool_scale, sgu_norm_g=sgu_norm_g, sgu_w=sgu_w, sgu_b=sgu_b)
    sh = _prep_shared(inp)
    x = np.asarray(x, np.float32); ctx = np.asarray(ctx, np.float32)
    c = np.asarray(c, np.float32); c_ctx = np.asarray(c_ctx, np.float32)
    cores = list(range(x.shape[0])) if _cores is None else _cores
    in_maps = []
    for b in cores:
        mcore = dict(sh)
        mcore["xT"] = np.ascontiguousarray(x[b].T)
        mcore["ctxT"] = np.ascontiguousarray(ctx[b].T)
        cc = np.stack([_fmvec(c[b]), _fmvec(c_ctx)], axis=2).reshape(128, 32)
        mcore["cc"] = np.ascontiguousarray(cc)
        in_maps.append(mcore)
    key = _stop_after
    if key not in _NC_CACHE:
        _NC_CACHE[key] = build_program(_stop_after)
    nc = _NC_CACHE[key]
    if _trace:
        res = run_bass_kernel_spmd(nc, in_maps, core_ids=list(range(len(cores))), trace=True)
        print("exec_time_ns", res.exec_time_ns)
    else:
        res = run_bass_kernel_spmd(nc, in_maps, core_ids=list(range(len(cores))))
    out = np.stack([np.ascontiguousarray(r["outT"].T) for r in res.results], axis=0)
    return out.astype(np.float32)
```

```python
import contextlib
import math
import numpy as np
import concourse.bass as bass
import concourse.mybir as mybir
from concourse.bass_utils import run_bass_kernel_spmd

F32 = mybir.dt.float32
BF16 = mybir.dt.bfloat16
AF = mybir.ActivationFunctionType
ALU = mybir.AluOpType
AX = mybir.AxisListType

D = 2048
L = 2048
LC = 256
DEPTH = 2
DFF = 5632
NJ = 44
EPS = 1e-6
TM = 256
FFN_TILES = [(0, 410), (410, 410), (820, 410), (1230, 409), (1639, 409)]
ENGS = ("pe", "act", "dve", "pool", "sp")


class Res:
    __slots__ = ("name", "w", "r")

    def __init__(self, name):
        self.name = name
        self.w = None
        self.r = []


class Lane:
    __slots__ = ("name", "sem", "cnt", "step")

    def __init__(self, name, sem, step):
        self.name, self.sem, self.cnt, self.step = name, sem, 0, step


class FW:
    def __init__(self, nc, sems, n_dma_lanes):
        self.nc = nc
        self.ops = {e: [] for e in ENGS}
        self.lanes = {}
        it = iter(sems)
        for e in ("pe", "act", "dve", "pool"):
            self.lanes[e] = Lane(e, next(it), 1)
        self.dma_lanes = [Lane("dma%d" % i, next(it), 16) for i in range(n_dma_lanes)]
        half = n_dma_lanes // 2
        self.q_lanes = {"pool": self.dma_lanes[:half], "sp": self.dma_lanes[half:]}
        self.dma_rr = {"pool": 0, "sp": 0}
        self.seen = {e: {} for e in ENGS}

    def _deps(self, reads, writes):
        deps = {}

        def add(d):
            if d is None:
                return
            ln, idx = d
            if deps.get(ln, 0) < idx:
                deps[ln] = idx
        for r in reads:
            add(r.w)
        for w in writes:
            add(w.w)
            for d in w.r:
                add(d)
        return deps

    def _emit_waits(self, eng, deps, own_lane):
        for ln, idx in deps.items():
            if ln is own_lane and eng == "pe":
                continue
            if self.seen[eng].get(ln, 0) >= idx:
                continue
            self.seen[eng][ln] = idx
            self.ops[eng].append(lambda e, sem=ln.sem, val=idx * ln.step: e.wait_ge(sem, val))

    def _mark(self, lane, reads, writes):
        idx = lane.cnt
        for r in reads:
            r.r.append((lane, idx))
            if len(r.r) > 64:
                best = {}
                for ln, i in r.r:
                    if best.get(ln, 0) < i:
                        best[ln] = i
                r.r = list(best.items())
        for w in writes:
            w.w = (lane, idx)
            w.r = []

    def op(self, eng, fn, reads=(), writes=()):
        lane = self.lanes[eng]
        self._emit_waits(eng, self._deps(reads, writes), lane)
        lane.cnt += 1
        self.ops[eng].append(lambda e, fn=fn, sem=lane.sem: fn(e).then_inc(sem, 1))
        self._mark(lane, reads, writes)

    def dma(self, eng, fn, reads=(), writes=()):
        ql = self.q_lanes[eng]
        lane = ql[self.dma_rr[eng] % len(ql)]
        self.dma_rr[eng] += 1
        deps = self._deps(reads, writes)
        if lane.cnt > 0 and deps.get(lane, 0) < lane.cnt:
            deps[lane] = lane.cnt
        self._emit_waits(eng, deps, None)
        lane.cnt += 1
        self.ops[eng].append(lambda e, fn=fn, sem=lane.sem: fn(e).then_inc(sem, 16))
        self._mark(lane, reads, writes)

    def barrier(self):
        lanes = list(self.lanes.values()) + self.dma_lanes
        for eng in ENGS:
            for ln in lanes:
                if ln.cnt == 0 or self.seen[eng].get(ln, 0) >= ln.cnt:
                    continue
                if ln is self.lanes.get(eng) and eng == "pe":
                    continue
                self.seen[eng][ln] = ln.cnt
                self.ops[eng].append(lambda e, sem=ln.sem, val=ln.cnt * ln.step: e.wait_ge(sem, val))

    def wait_all_dma(self, eng):
        for ln in self.dma_lanes:
            if ln.cnt > 0:
                self.ops[eng].append(lambda e, sem=ln.sem, val=ln.cnt * 16: e.wait_ge(sem, val))

    def replay(self, block):
        m = {"pe": block.tensor, "act": block.scalar, "dve": block.vector,
             "pool": block.gpsimd, "sp": block.sync}
        for e in ENGS:
            ops = self.ops[e]
            if not ops:
                continue

            def body(engine, ops=ops):
                for f in ops:
                    f(engine)
            m[e](body)


def build_program(stop_after=None):
    nc = bass.Bass("TRN2", target_bir_lowering=False)

    def din(name, shape, dtype=F32):
        return nc.dram_tensor(name, list(shape), dtype, kind="ExternalInput").ap()

    xT_d = din("xT", [D, L])
    ctxT_d = din("ctxT", [D, LC])
    cc_d = din("cc", [128, 32])
    wada_d = din("wada", [DEPTH, 48, 128, 16, 256])
    bada_d = din("bada", [DEPTH, 128, 192])
    g1_d = din("g1", [DEPTH, 128, 16])
    g2_d = din("g2", [DEPTH, 128, 16])
    gf_d = din("gf", [128, 16])
    win_d = din("win", [DEPTH, 12, 128, 16, 256])
    wout_d = din("wout", [DEPTH, 8, 128, 16, 256])
    wup_d = din("wup", [DEPTH, NJ, 128, 16, 256])
    wdn_d = din("wdn", [DEPTH, 16, 2, 128, 22, 128])
    convw_d = din("convw", [DEPTH, 128, 352])
    qkg_d = din("qkg", [DEPTH, 128, 2])
    sink_d = din("sink", [DEPTH, 128, 8])
    poolw_d = din("poolw", [DEPTH, 128, 4, 128])
    poolsc_d = din("poolsc", [DEPTH, 128, 4])
    sgug_d = din("sgug", [DEPTH, 128, 512])
    sguw_d = din("sguw", [DEPTH, 128, 4, 128])
    sgub_d = din("sgub", [DEPTH, 128, 4, 128])
    cos_d = din("cosT", [128, L])
    sin_d = din("sinT", [128, L])
    rot_d = din("rotT", [128, 128])
    maskp_d = din("maskP", [128, 512])
    maskn_d = din("maskN", [128, 512])
    invl_d = din("invL", [128, 4, L])
    invc_d = din("invC", [128, 4, LC])
    out_d = nc.dram_tensor("outT", [D, L], F32, kind="ExternalOutput").ap()
    xm_d = nc.dram_tensor("xm", [D, L], F32, kind="Internal").ap()
    xa_d = nc.dram_tensor("xa", [D, L], F32, kind="Internal").ap()
    cm_d = nc.dram_tensor("cm", [D, LC], F32, kind="Internal").ap()
    ca_d = nc.dram_tensor("ca", [D, LC], F32, kind="Internal").ap()

    win_b = nc.dram_tensor("win_b", [DEPTH, 12, 128, 16, 256], BF16, kind="Internal").ap()
    wout_b = nc.dram_tensor("wout_b", [DEPTH, 8, 128, 16, 256], BF16, kind="Internal").ap()
    wup_b = nc.dram_tensor("wup_b", [DEPTH, NJ, 128, 16, 256], BF16, kind="Internal").ap()
    wdn_b = nc.dram_tensor("wdn_b", [DEPTH, 16, 2, 128, 22, 128], BF16, kind="Internal").ap()

    def fm(ap):
        return ap.rearrange("(c p) t -> p c t", p=128)

    es = contextlib.ExitStack()
    with es:
        N_DMA_LANES = 16
        sems = [es.enter_context(nc.semaphore("s%d" % i)) for i in range(4 + N_DMA_LANES)]

        def sb(name, shape, dtype=F32):
            return es.enter_context(nc.sbuf_tensor(name, list(shape), dtype))

        fw = FW(nc, sems, N_DMA_LANES)
        R = lambda n: Res(n)

        xbuf0 = sb("xbuf", [128, 16, 512]); hbuf0 = sb("hbuf", [128, 16, 512], BF16)
        r_xs = [[R("x%d_%d" % (i, c)) for c in range(16)] for i in range(2)]
        r_hs = [[R("h%d_%d" % (i, c)) for c in range(16)] for i in range(2)]
        NWR = 3
        wr = [sb("wr%d" % i, [128, 16, 256], BF16) for i in range(NWR)]; r_wr = [R("wr%d" % i) for i in range(NWR)]
        NTF = 4
        tf_ = [sb("tf%d" % i, [128, 512]) for i in range(NTF)]; r_tf = [R("tf%d" % i) for i in range(NTF)]
        NTB = 3
        tb_ = [sb("tb%d" % i, [128, 512], BF16) for i in range(NTB)]; r_tb = [R("tb%d" % i) for i in range(NTB)]
        kcT = sb("kcT", [128, 2, 256], BF16); r_kc = [R("kc0"), R("kc1")]
        vcS = sb("vcS", [128, 2, 256], BF16); r_vc = R("vc")
        ARENA_BYTES = 108288 + 1536
        arena = sb("arena", [128, ARENA_BYTES // 4])

        class Lay:
            def __init__(self):
                self.off = 0

            def get(self, shape, dtype=F32):
                n = 1
                for d_ in shape[1:]:
                    n *= d_
                nbytes = n * (4 if dtype == F32 else 2)
                assert self.off % 4 == 0 and nbytes % 4 == 0
                ap = arena[:, self.off // 4:(self.off + nbytes) // 4]
                if dtype != F32:
                    ap = ap.bitcast(dtype)
                if len(shape) == 3:
                    ap = ap.rearrange("p (a b) -> p a b", a=shape[1])
                elif len(shape) == 4:
                    ap = ap.rearrange("p (a b c) -> p a b c", a=shape[1], b=shape[2])
                self.off += nbytes
                assert self.off <= ARENA_BYTES, self.off
                return ap
        lm = Lay()
        xbuf1 = lm.get([128, 16, 512]); hbuf1 = lm.get([128, 16, 512], BF16)
        common_off = lm.off
        mixT = lm.get([128, 16, TM], BF16); r_mix = [R("mix%d" % c) for c in range(16)]
        qT = lm.get([128, 2, 2, 512], BF16); r_q = [[R("q%d%d" % (a, b)) for b in range(2)] for a in range(2)]
        kT = lm.get([128, 2, 512], BF16); r_k = [R("k0"), R("k1")]
        vE = lm.get([128, 4, 256], BF16); r_v = R("vE")
        WP = 544
        pvE = lm.get([128, 4, WP]); r_pv = [R("pv%d" % g) for g in range(4)]
        pA = lm.get([128, WP]); pB = lm.get([128, WP]); r_pA, r_pB = R("pA"), R("pB")
        ybf = [lm.get([128, TM], BF16) for i in range(4)]; r_y = [R("ybf%d" % i) for i in range(4)]
        ug = lm.get([128, 4, TM]); r_ug = [R("ug%d" % g) for g in range(4)]
        gtm = lm.get([128, 2, 512]); r_gtm = [R("gtm0"), R("gtm1")]
        vb = lm.get([128, 2, 512], BF16); r_vb = [R("vb0"), R("vb1")]
        cosS = lm.get([128, 512]); sinS = lm.get([128, 512]); r_rope = R("rope")
        NPT = 10
        pT = [lm.get([128, 512], BF16) for i in range(NPT)]; r_pT = [R("pT%d" % i) for i in range(NPT)]
        invS = lm.get([128, 4, TM]); r_inv = R("inv")
        rdn = lm.get([128, 128]); r_rdn = R("rdn")
        lf = Lay(); lf.off = common_off
        actT = lf.get([128, NJ, 412], BF16); r_act = [R("a%d" % j) for j in range(NJ)]
        NDN = 3
        dn = [lf.get([128, 22, 128], BF16) for i in range(NDN)]; r_dn = [R("dn%d" % i) for i in range(NDN)]
        xbufs = [xbuf0, xbuf1]; hbufs = [hbuf0, hbuf1]
        ssg = sb("ssg", [128, 2]); r_ssg = R("ssg")
        rsg = sb("rsg", [128, 2]); r_rsg = R("rsg")
        ones_bf = sb("ones_bf", [128, 128], BF16); r_ones = R("ones")
        rot_bf = sb("rot_bf", [128, 128], BF16); r_rot = R("rot")
        maskP = sb("maskPs", [128, 512], BF16); maskN = sb("maskNs", [128, 512], BF16); r_mask = R("mask")
        cc_sb = sb("cc_sb", [128, 32]); scc = sb("scc", [128, 32], BF16); r_cc = R("cc"); r_scc = R("scc")
        mods = [sb("mod%d" % i, [128, 192]) for i in range(DEPTH)]; r_mods = [R("mod%d" % i) for i in range(DEPTH)]
        bada = sb("bada_s", [128, 192]); r_bada = R("bada")
        g1s = sb("g1s", [128, 16]); g2s = sb("g2s", [128, 16]); gfs = sb("gfs", [128, 16]); r_g = R("g")
        A1s = [sb("A1_%d" % i, [128, 32]) for i in range(DEPTH)]; A2s = [sb("A2_%d" % i, [128, 32]) for i in range(DEPTH)]
        r_As = [R("A%d" % i) for i in range(DEPTH)]
        convw = sb("convw_s", [128, 352]); r_cw = R("cw")
        qkg = sb("qkg_s", [128, 2]); r_qkg = R("qkg")
        sinkS = sb("sinkS", [128, 8]); sinkE = sb("sinkE", [128, 8]); r_sink = R("sink")
        poolw_bf = sb("poolw_bf", [128, 4, 128], BF16); r_pw = R("poolw")
        poolsc = sb("poolsc_s", [128, 4]); r_psc = R("poolsc")
        sgug = sb("sgug_s", [128, 512]); r_sgug = R("sgug")
        sguw_bf = sb("sguw_bf", [128, 4, 128], BF16); r_sguw = R("sguw")
        sgub = sb("sgub_s", [128, 4, 128]); r_sgub = R("sgub")
        epsc = sb("epsc", [128, 4]); r_eps = R("eps")
        rsbuf = sb("rsbuf", [128, 512]); r_rsbuf = R("rsbuf")

        banks = [es.enter_context(nc.psum_tensor("ps%d" % i, [128, 512], F32)) for i in range(8)]
        r_bank = [R("bank%d" % i) for i in range(8)]
        st = {"bank": 0, "tf": 0, "tb": 0, "wr": 0, "dn": 0, "pT": 0}

        def nbank():
            i = st["bank"] % 7; st["bank"] += 1
            return banks[i], r_bank[i]

        def ntf():
            i = st["tf"] % NTF; st["tf"] += 1
            return tf_[i], r_tf[i]

        def ntb():
            i = st["tb"] % NTB; st["tb"] += 1
            return tb_[i], r_tb[i]

        def nwr():
            i = st["wr"] % NWR; st["wr"] += 1
            return wr[i], r_wr[i]

        def ndn():
            i = st["dn"] % NDN; st["dn"] += 1
            return dn[i], r_dn[i]

        def npt():
            i = st["pT"] % NPT; st["pT"] += 1
            return pT[i], r_pT[i]

        block = es.enter_context(nc.Block())

        def mmgroup(out_ap, pairs, reads, writes, first=True, last=True):
            def f(e):
                ins = None
                n = len(pairs)
                for i, (a, b) in enumerate(pairs):
                    ins = e.matmul(out_ap, lhsT=a, rhs=b, start=(first and i == 0), stop=(last and i == n - 1))
                return ins
            fw.op("pe", f, reads, writes)

        def load(dst_ap, src_ap, res, eng="sp", reads=()):
            fw.dma(eng, lambda e: e.dma_start(out=dst_ap, in_=src_ap), reads=reads, writes=res)

        def rsqrt_(out_ap, in_ap, eps_col, reads, wres):
            fw.op("act", lambda e: e.activation(out=out_ap, in_=in_ap, func=AF.Ln, bias=epsc[:, eps_col:eps_col + 1], scale=1.0),
                  reads=list(reads) + [r_eps], writes=[wres])
            fw.op("act", lambda e: e.activation(out=out_ap, in_=out_ap, func=AF.Exp, scale=-0.5), reads=[wres], writes=[wres])

        r_win = [[R("win%d_%d" % (l, i)) for i in range(12)] for l in range(DEPTH)]
        r_wout = [[R("wout%d_%d" % (l, i)) for i in range(8)] for l in range(DEPTH)]
        r_wup = [[R("wup%d_%d" % (l, i)) for i in range(NJ)] for l in range(DEPTH)]
        r_wdn = [[[R("wdn%d_%d_%d" % (l, i, h)) for h in range(2)] for i in range(16)] for l in range(DEPTH)]
        cached = set()

        def wload(wt, rw, src_f32, scratch_bf, r_scr):
            key = id(r_scr)
            if key in cached:
                load(wt[:], scratch_bf, [rw], eng="pool", reads=[r_scr])
            else:
                load(wt[:], src_f32, [rw], eng="pool")
                load(scratch_bf, wt[:], [r_scr], eng="pool", reads=[rw])
                cached.add(key)

        fw.op("dve", lambda e: e.memset(epsc[:, 0:1], float(D) * EPS), writes=[r_eps])
        fw.op("dve", lambda e: e.memset(epsc[:, 1:2], 128.0 * EPS), writes=[r_eps])
        fw.op("dve", lambda e: e.memset(epsc[:, 2:3], 512.0 * EPS), writes=[r_eps])
        t0_, rt0 = ntf()
        fw.op("dve", lambda e: e.memset(t0_[:, 0:128], 1.0), writes=[rt0])
        fw.op("dve", lambda e: e.tensor_copy(out=ones_bf[:], in_=t0_[:, 0:128]), reads=[rt0], writes=[r_ones])
        load(rot_bf[:], rot_d, [r_rot], eng="pool")
        load(maskP[:], maskp_d, [r_mask], eng="pool")
        load(maskN[:], maskn_d, [r_mask], eng="pool")
        load(cc_sb[:], cc_d, [r_cc])
        fw.op("act", lambda e: e.activation(out=scc[:], in_=cc_sb[:], func=AF.Silu), reads=[r_cc], writes=[r_scc])
        load(gfs[:], gf_d, [r_g])

        SQD = math.sqrt(float(D))

        def layer_prologue(l):
            load(convw[:], convw_d[l], [r_cw])
            load(qkg[:], qkg_d[l], [r_qkg])
            load(sinkS[:], sink_d[l], [r_sink])
            load(poolsc[:], poolsc_d[l], [r_psc])
            load(sgug[:], sgug_d[l], [r_sgug])
            load(sgub[:], sgub_d[l], [r_sgub])
            load(poolw_bf[:], poolw_d[l], [r_pw], eng="pool")
            load(sguw_bf[:], sguw_d[l], [r_sguw], eng="pool")
            fw.op("act", lambda e: e.activation(out=sinkE[:], in_=sinkS[:], func=AF.Exp), reads=[r_sink], writes=[r_sink])
            fw.op("dve", lambda e: e.tensor_scalar(out=sgug[:], in0=sgug[:], scalar1=math.sqrt(512.0), scalar2=None, op0=ALU.mult),
                  reads=[r_sgug], writes=[r_sgug])

        def prologue_mod(l):
            mod, A1, A2, r_mod, r_A = mods[l], A1s[l], A2s[l], r_mods[l], r_As[l]
            load(bada[:], bada_d[l], [r_bada])
            load(g1s[:], g1_d[l], [r_g])
            load(g2s[:], g2_d[l], [r_g])
            fw.op("dve", lambda e: e.tensor_scalar(out=g1s[:], in0=g1s[:], scalar1=SQD, scalar2=None, op0=ALU.mult), reads=[r_g], writes=[r_g])
            fw.op("dve", lambda e: e.tensor_scalar(out=g2s[:], in0=g2s[:], scalar1=SQD, scalar2=None, op0=ALU.mult), reads=[r_g], writes=[r_g])
            bk, rb = banks[7], r_bank[7]
            for pp in range(48):
                wt, rw = nwr()
                load(wt[:], wada_d[l, pp], [rw], eng="pool")
                pairs = []
                for half in range(2):
                    n = 2 * pp + half
                    pairs.append((n, [(wt[:, k, half * 128:(half + 1) * 128], scc[:, 2 * k:2 * k + 2]) for k in range(16)]))

                def f(e, pairs=pairs):
                    ins = None
                    for n, pr in pairs:
                        for i, (a, b) in enumerate(pr):
                            ins = e.matmul(bk[:, 2 * n:2 * n + 2], lhsT=a, rhs=b, start=(i == 0), stop=(i == 15))
                    return ins
                fw.op("pe", f, reads=[rw, r_scc], writes=[rb])
                yield
            fw.op("dve", lambda e: e.tensor_tensor(out=mod[:], in0=bk[:, 0:192], in1=bada[:], op=ALU.add),
                  reads=[rb, r_bada], writes=[r_mod])
            for c in range(16):
                for m in range(2):
                    def fa(e, c=c, m=m):
                        return e.tensor_scalar(out=A1[:, 2 * c + m:2 * c + m + 1], in0=mod[:, (16 + c) * 2 + m:(16 + c) * 2 + m + 1],
                                               scalar1=1.0, scalar2=g1s[:, c:c + 1], op0=ALU.add, op1=ALU.mult)
                    fw.op("dve", fa, reads=[r_mod, r_g], writes=[r_A])

                    def fb(e, c=c, m=m):
                        return e.tensor_scalar(out=A2[:, 2 * c + m:2 * c + m + 1], in0=mod[:, (64 + c) * 2 + m:(64 + c) * 2 + m + 1],
                                               scalar1=1.0, scalar2=g2s[:, c:c + 1], op0=ALU.add, op1=ALU.mult)
                    fw.op("dve", fb, reads=[r_mod, r_g], writes=[r_A])

        hook = {"gen": None}

        def step_hook():
            g_ = hook["gen"]
            if g_ is not None:
                try:
                    next(g_)
                except StopIteration:
                    hook["gen"] = None

        def drain_hook():
            while hook["gen"] is not None:
                step_hook()

        def modcol(l, i, c, m):
            col = (i * 16 + c) * 2 + m
            return mods[l][:, col:col + 1]

        def norm_mod(l, s_, lo, hi, Acoef, shift_i, m):
            r_mod, r_A = r_mods[l], r_As[l]
            xbuf, hbuf, r_x, r_h = xbufs[s_], hbufs[s_], r_xs[s_], r_hs[s_]
            W = hi - lo
            for c in range(16):
                fw.op("act", lambda e, c=c: e.activation(out=hbuf[:, c, lo:hi], in_=xbuf[:, c, lo:hi], func=AF.Square),
                      reads=[r_x[c]], writes=[r_h[c]])
            bk, rb = nbank()
            mmgroup(bk[:, 0:W], [(ones_bf[:], hbuf[:, c, lo:hi]) for c in range(16)], reads=r_h + [r_ones], writes=[rb])
            rs, rrs = rsbuf, r_rsbuf
            rsqrt_(rs[:, 0:W], bk[:, 0:W], 0, [rb], rrs)
            for c in range(16):
                t, rt = ntf()
                fw.op("dve", lambda e, c=c, t=t: e.scalar_tensor_tensor(out=t[:, 0:W], in0=xbuf[:, c, lo:hi],
                                                                        scalar=Acoef[:, 2 * c + m:2 * c + m + 1], in1=rs[:, 0:W],
                                                                        op0=ALU.mult, op1=ALU.mult),
                      reads=[r_x[c], rrs, r_A], writes=[rt])
                fw.op("act", lambda e, c=c, t=t: e.activation(out=hbuf[:, c, lo:hi], in_=t[:, 0:W], func=AF.Identity,
                                                              bias=modcol(l, shift_i, c, m), scale=1.0),
                      reads=[rt, r_mod], writes=[r_h[c]])

        def head_norm_rope(bk, rb, W, gcol, rope, rope_c0, dests):
            sq, rsq = ntb()
            fw.op("act", lambda e: e.activation(out=sq[:, 0:W], in_=bk[:, 0:W], func=AF.Square), reads=[rb], writes=[rsq])
            b2, rb2 = nbank()
            mmgroup(b2[:, 0:W], [(ones_bf[:], sq[:, 0:W])], reads=[rsq, r_ones], writes=[rb2])
            rs, rrs = ntf()
            rsqrt_(rs[:, 0:W], b2[:, 0:W], 1, [rb2], rrs)
            if not rope:
                for (a, b, dap, dres) in dests:
                    fw.op("dve", lambda e, a=a, b=b, dap=dap: e.scalar_tensor_tensor(
                        out=dap, in0=bk[:, a:b], scalar=qkg[:, gcol:gcol + 1], in1=rs[:, a:b], op0=ALU.mult, op1=ALU.mult),
                        reads=[rb, rrs, r_qkg], writes=[dres])
                return
            qn, rqn = ntb()
            fw.op("dve", lambda e: e.scalar_tensor_tensor(out=qn[:, 0:W], in0=bk[:, 0:W], scalar=qkg[:, gcol:gcol + 1],
                                                          in1=rs[:, 0:W], op0=ALU.mult, op1=ALU.mult),
                  reads=[rb, rrs, r_qkg], writes=[rqn])

            def stage2():
                b3, rb3 = nbank()
                mmgroup(b3[:, 0:W], [(rot_bf[:], qn[:, 0:W])], reads=[rqn, r_rot], writes=[rb3])
                t1, rt1 = ntf()
                fw.op("dve", lambda e: e.tensor_tensor(out=t1[:, 0:W], in0=qn[:, 0:W], in1=cosS[:, rope_c0:rope_c0 + W], op=ALU.mult),
                      reads=[rqn, r_rope], writes=[rt1])
                t2, rt2 = ntf()
                fw.op("dve", lambda e: e.tensor_tensor(out=t2[:, 0:W], in0=b3[:, 0:W], in1=sinS[:, rope_c0:rope_c0 + W], op=ALU.mult),
                      reads=[rb3, r_rope], writes=[rt2])
                for (a, b, dap, dres) in dests:
                    fw.op("dve", lambda e, a=a, b=b, dap=dap: e.tensor_tensor(out=dap, in0=t1[:, a:b], in1=t2[:, a:b], op=ALU.add),
                          reads=[rt1, rt2], writes=[dres])
            return stage2

        def geo(t0, T, Lseq, m):
            lat = (m == 0)
            if lat:
                e0 = max(t0 - 128, 0); e1 = min(t0 + T + 128, Lseq)
            else:
                e0, e1 = 0, Lseq
            E = e1 - e0
            return lat, e0, e1, E, t0 - e0, T // 128, E // 128

        def mixer_prep(s_, l, src, dst, r_src, r_dst, t0, T, Lseq, m, full):
            xbuf, hbuf, r_x, r_h = xbufs[s_], hbufs[s_], r_xs[s_], r_hs[s_]
            lat, e0, e1, E, c0, nblk, nblk_e = geo(t0, T, Lseq, m)
            fw.dma("sp", lambda e: e.dma_start(out=xbuf[:, :, 0:E], in_=fm(src)[:, :, e0:e1]), reads=r_src, writes=r_x)
            norm_mod(l, s_, 0, E, A1s[l], 0, m)

        def mixer_body1(s_, l, src, dst, r_src, r_dst, t0, T, Lseq, m, full):
            xbuf, hbuf, r_x, r_h = xbufs[s_], hbufs[s_], r_xs[s_], r_hs[s_]
            lat, e0, e1, E, c0, nblk, nblk_e = geo(t0, T, Lseq, m)
            if lat:
                load(cosS[:, 0:E], cos_d[:, e0:e1], [r_rope])
                load(sinS[:, 0:E], sin_d[:, e0:e1], [r_rope])
            if full:
                load(invS[:, :, 0:T], (invl_d if lat else invc_d)[:, :, t0:t0 + T], [r_inv])
            if full:
                fw.op("dve", lambda e: e.memset(pvE[:], 0.0), writes=r_pv)
            kdst = kT if lat else kcT
            r_kdst = r_k if lat else r_kc
            vdst = vE if lat else vcS
            r_vdst = r_v if lat else r_vc
            pair_list = list(range(12)) if full else [4, 5]
            pend = []
            pend2 = []

            def defer(fn):
                while pend2:
                    pend2.pop(0)()
                if pend:
                    r_ = pend.pop(0)()
                    if r_ is not None:
                        pend2.append(r_)
                if fn is not None:
                    pend.append(fn)
            for pp in pair_list:
                wt, rw = nwr()
                wload(wt, rw, win_d[l, pp], win_b[l, pp], r_win[l][pp])
                for half in range(2):
                    j = 2 * pp + half
                    wsl = lambda k, half=half, wt=wt: wt[:, k, half * 128:(half + 1) * 128]
                    if j < 8 or 16 <= j < 20:
                        bk, rb = nbank()
                        mmgroup(bk[:, 0:T], [(wsl(k), hbuf[:, k, c0:c0 + T]) for k in range(16)], reads=r_h + [rw], writes=[rb])
                        if j < 8:
                            kh, g = j // 4, j % 4
                            dests = [(n * 128, (n + 1) * 128, qT[:, kh, n, g * 128:(g + 1) * 128], r_q[kh][n]) for n in range(nblk)]
                            defer(lambda bk=bk, rb=rb, dests=dests: head_norm_rope(bk, rb, T, 0, lat, c0, dests))
                        else:
                            gi = j - 16
                            defer(lambda gi=gi, bk=bk, rb=rb: fw.op(
                                "act", lambda e: e.activation(out=ug[:, gi, 0:T], in_=bk[:, 0:T], func=AF.Gelu),
                                reads=[rb], writes=[r_ug[gi]]))
                    elif 8 <= j < 10 or 12 <= j < 16:
                        bk, rb = nbank()
                        mmgroup(bk[:, 0:E], [(wsl(k), hbuf[:, k, 0:E]) for k in range(16)], reads=r_h + [rw], writes=[rb])
                        if j < 10:
                            kh = j - 8
                            defer(lambda bk=bk, rb=rb, kh=kh: head_norm_rope(bk, rb, E, 1, lat, 0, [(0, E, kdst[:, kh, 0:E], r_kdst[kh])]))
                        else:
                            gi = j - 12
                            defer(lambda gi=gi, bk=bk, rb=rb: fw.op(
                                "act", lambda e: e.activation(out=pvE[:, gi, 16:16 + E], in_=bk[:, 0:E], func=AF.Copy),
                                reads=[rb], writes=[r_pv[gi]]))
                    else:
                        isv = j < 12
                        nb_ = nblk_e if isv else nblk
                        cb = 0 if isv else c0
                        bk, rb = nbank()

                        def f(e, nb_=nb_, cb=cb, wsl=wsl, bk=bk):
                            ins = None
                            for b in range(nb_):
                                for k in range(16):
                                    ins = e.matmul(bk[:, b * 128:(b + 1) * 128], lhsT=hbuf[:, k, cb + b * 128:cb + (b + 1) * 128],
                                                   rhs=wsl(k), start=(k == 0), stop=(k == 15))
                            return ins
                        fw.op("pe", f, reads=r_h + [rw], writes=[rb])
                        def evac(nb_=nb_, isv=isv, j=j, bk=bk, rb=rb):
                            for b in range(nb_):
                                if isv:
                                    off = (j - 10) * 128
                                    fw.op("act", lambda e, b=b, off=off: e.activation(out=vdst[:, b, off:off + 128],
                                                                                      in_=bk[:, b * 128:(b + 1) * 128], func=AF.Copy),
                                          reads=[rb], writes=[r_vdst])
                                else:
                                    off = (j - 20) * 128
                                    fw.op("act", lambda e, b=b, off=off: e.activation(out=gtm[:, b, off:off + 128],
                                                                                      in_=bk[:, b * 128:(b + 1) * 128], func=AF.Gelu),
                                          reads=[rb], writes=[r_gtm[b]])
                        defer(evac)
            defer(None)
            defer(None)
            assert not pend and not pend2
            if not full:
                return
            SC = math.sqrt(128.0)
            def attn_front(kh, n):
                keys = []
                if lat:
                    nbg = t0 // 128 + n
                    be = c0 // 128 + n
                    if nbg > 0:
                        keys.append((kT[:, kh, (be - 1) * 128:be * 128], vE[:, be - 1, kh * 128:(kh + 1) * 128], maskP, [r_k[kh], r_v]))
                    keys.append((kT[:, kh, be * 128:(be + 1) * 128], vE[:, be, kh * 128:(kh + 1) * 128], None, [r_k[kh], r_v]))
                    if nbg < Lseq // 128 - 1:
                        keys.append((kT[:, kh, (be + 1) * 128:(be + 2) * 128], vE[:, be + 1, kh * 128:(kh + 1) * 128], maskN, [r_k[kh], r_v]))
                for b in range(2):
                    keys.append((kcT[:, kh, b * 128:(b + 1) * 128], vcS[:, b, kh * 128:(kh + 1) * 128], None, [r_kc[kh], r_vc]))
                ps = []
                for (kap, vap, msk, rr) in keys:
                    bs, rbs = nbank()
                    mmgroup(bs[:, :], [(kap, qT[:, kh, n, :])], reads=rr + [r_q[kh][n]], writes=[rbs])
                    p, rp = npt()
                    fw.op("act", lambda e, p=p, bs=bs: e.activation(out=p[:], in_=bs[:, :], func=AF.Exp, scale=SC),
                          reads=[rbs], writes=[rp])
                    if msk is not None:
                        fw.op("dve", lambda e, p=p, msk=msk: e.tensor_tensor(out=p[:], in0=p[:], in1=msk[:], op=ALU.mult),
                              reads=[rp, r_mask], writes=[rp])
                    ps.append((p, rp, vap, rr))
                return ps

            def attn_back(kh, n, ps):
                bo, rbo = nbank()
                bd, rbd = nbank()
                nk = len(ps)
                for i, (p, rp, vap, rr) in enumerate(ps):
                    mmgroup(bo[:, :], [(vap, p[:])], reads=rr + [rp], writes=[rbo], first=(i == 0), last=(i == nk - 1))
                    mmgroup(bd[:, :], [(ones_bf[:], p[:])], reads=[rp, r_ones], writes=[rbd], first=(i == 0), last=(i == nk - 1))
                rd, rrd = ntf()
                for g in range(4):
                    hq = kh * 4 + g
                    fw.op("act", lambda e, g=g, hq=hq: e.activation(out=rd[:, g * 128:(g + 1) * 128], in_=bd[:, g * 128:(g + 1) * 128],
                                                                     func=AF.Ln, bias=sinkE[:, hq:hq + 1], scale=1.0),
                          reads=[rbd, r_sink], writes=[rrd])
                fw.op("act", lambda e: e.activation(out=rd[:], in_=rd[:], func=AF.Exp, scale=-1.0), reads=[rrd], writes=[rrd])
                fw.op("dve", lambda e: e.tensor_tensor(out=mixT[:, 4 * kh:4 * kh + 4, n * 128:(n + 1) * 128],
                                                       in0=bo[:, :].rearrange("p (g q) -> p g q", g=4),
                                                       in1=rd[:].rearrange("p (g q) -> p g q", g=4), op=ALU.mult),
                      reads=[rbo, rrd], writes=r_mix[4 * kh:4 * kh + 4])
            pc0 = 16 + c0

            def pool_dve(gi):
                cur = None
                bufs = [(pA, r_pA), (pB, r_pB)]
                for lv in range(1, gi + 2):
                    hw = 1 << (lv - 1)
                    if lv == 1:
                        lo_, hi_ = 1, WP - 1
                        o, ro = bufs[0]
                        fw.op("dve", lambda e, o=o, lo_=lo_, hi_=hi_: e.tensor_tensor(
                            out=o[:, lo_:hi_], in0=pvE[:, gi, lo_ - 1:hi_ - 1], in1=pvE[:, gi, lo_:hi_], op=ALU.add),
                            reads=[r_pv[gi]], writes=[ro])
                        cur = (o, ro)
                    else:
                        sh = hw // 2
                        lo_, hi_ = hw, WP - hw
                        o, ro = bufs[(lv - 1) % 2]
                        ci, rci = cur
                        fw.op("dve", lambda e, o=o, ci=ci, lo_=lo_, hi_=hi_, sh=sh: e.tensor_tensor(
                            out=o[:, lo_:hi_], in0=ci[:, lo_ - sh:hi_ - sh], in1=ci[:, lo_ + sh:hi_ + sh], op=ALU.add),
                            reads=[rci], writes=[ro])
                        cur = (o, ro)
                s2, rs_ = cur
                t, rt = ntf()
                fw.op("dve", lambda e: e.tensor_tensor(out=t[:, 0:T], in0=s2[:, pc0:pc0 + T], in1=invS[:, gi, 0:T], op=ALU.mult),
                      reads=[rs_, r_inv], writes=[rt])
                fw.op("dve", lambda e: e.tensor_tensor(out=ybf[gi][:, 0:T], in0=t[:, 0:T], in1=pvE[:, gi, pc0:pc0 + T], op=ALU.subtract),
                      reads=[rt, r_pv[gi]], writes=[r_y[gi]])

            def pool_pe(gi):
                bk, rb = nbank()
                mmgroup(bk[:, 0:T], [(poolw_bf[:, gi, :], ybf[gi][:, 0:T])], reads=[r_pw, r_y[gi]], writes=[rb])
                fw.op("act", lambda e: e.activation(out=mixT[:, 8 + gi, 0:T], in_=bk[:, 0:T], func=AF.Identity,
                                                    scale=poolsc[:, gi:gi + 1]),
                      reads=[rb, r_psc], writes=[r_mix[8 + gi]])

            def sgu_pre():
                for b in range(nblk):
                    t, rt = ntf()
                    fw.op("act", lambda e, t=t, b=b: e.activation(out=t[:], in_=gtm[:, b, :], func=AF.Square), reads=[r_gtm[b]], writes=[rt])
                    fw.op("dve", lambda e, t=t, b=b: e.tensor_reduce(out=ssg[:, b:b + 1], in_=t[:], axis=AX.X, op=ALU.add),
                          reads=[rt], writes=[r_ssg])
                rsqrt_(rsg[:, 0:nblk], ssg[:, 0:nblk], 2, [r_ssg], r_rsg)
                for b in range(nblk):
                    fw.op("dve", lambda e, b=b: e.scalar_tensor_tensor(out=vb[:, b, :], in0=gtm[:, b, :], scalar=rsg[:, b:b + 1],
                                                                       in1=sgug[:], op0=ALU.mult, op1=ALU.mult),
                          reads=[r_gtm[b], r_rsg, r_sgug], writes=[r_vb[b]])

            def sgu_pe(h):
                bk, rb = nbank()

                def f(e):
                    ins = None
                    for b in range(nblk):
                        ins = e.matmul(bk[:, b * 128:(b + 1) * 128], lhsT=vb[:, b, h * 128:(h + 1) * 128], rhs=sguw_bf[:, h, :],
                                       start=True, stop=True)
                    return ins
                fw.op("pe", f, reads=r_vb[:nblk] + [r_sguw], writes=[rb])
                t, rt = ntf()
                for b in range(nblk):
                    fw.op("dve", lambda e, b=b: e.tensor_tensor(out=t[:, b * 128:(b + 1) * 128], in0=bk[:, b * 128:(b + 1) * 128],
                                                                in1=sgub[:, h, :], op=ALU.add),
                          reads=[rb, r_sgub], writes=[rt])
                fw.op("dve", lambda e: e.tensor_tensor(out=mixT[:, 12 + h, 0:T], in0=t[:, 0:T], in1=ug[:, h, 0:T], op=ALU.mult),
                      reads=[rt, r_ug[h]], writes=[r_mix[12 + h]])

            its = [(kh, n) for kh in range(2) for n in range(nblk)]
            sgu_pre()
            front = attn_front(*its[0])
            for ii, (kh, n) in enumerate(its):
                nxt = attn_front(*its[ii + 1]) if ii + 1 < len(its) else None
                if ii < 4:
                    pool_dve(ii)
                attn_back(kh, n, front)
                if ii < 4:
                    pool_pe(ii)
                    sgu_pe(ii)
                front = nxt
            for ii in range(len(its), 4):
                pool_dve(ii); pool_pe(ii); sgu_pe(ii)

        def mixer_body2(s_, l, src, dst, r_src, r_dst, t0, T, Lseq, m, full):
            xbuf, hbuf, r_x, r_h = xbufs[s_], hbufs[s_], r_xs[s_], r_hs[s_]
            lat, e0, e1, E, c0, nblk, nblk_e = geo(t0, T, Lseq, m)
            if not full:
                return
            import os
            _br = os.environ.get("DBG_BR", "aps")
            for ch, lo_c, hi_c in (("a", 0, 8), ("p", 8, 12), ("s", 12, 16)):
                if ch not in _br:
                    for cc_ in range(lo_c, hi_c):
                        fw.op("dve", lambda e, cc_=cc_: e.memset(mixT[:, cc_, 0:T], 0.0), writes=[r_mix[cc_]])
            for pp in range(8):
                wt, rw = nwr()
                wload(wt, rw, wout_d[l, pp], wout_b[l, pp], r_wout[l][pp])
                for half in range(2):
                    n = 2 * pp + half
                    bk, rb = nbank()
                    mmgroup(bk[:, 0:T], [(wt[:, k, half * 128:(half + 1) * 128], mixT[:, k, 0:T]) for k in range(16)],
                            reads=r_mix + [rw], writes=[rb])
                    fw.op("dve", lambda e, n=n, bk=bk: e.scalar_tensor_tensor(out=xbuf[:, n, c0:c0 + T], in0=bk[:, 0:T], scalar=modcol(l, 2, n, m),
                                                                              in1=xbuf[:, n, c0:c0 + T], op0=ALU.mult, op1=ALU.add),
                          reads=[rb, r_mods[l], r_x[n]], writes=[r_x[n]])
            fw.dma("sp", lambda e: e.dma_start(out=fm(dst)[:, :, t0:t0 + T], in_=xbuf[:, :, c0:c0 + T]), reads=r_x, writes=r_dst)

        def ffn_prep(s_, l, src, dst, r_src, r_dst, t0, T, Lseq, m, final):
            xbuf, hbuf, r_x, r_h = xbufs[s_], hbufs[s_], r_xs[s_], r_hs[s_]
            E = T + 2
            lo = 1 if t0 == 0 else 0
            hi = E - 1 if t0 + T == Lseq else E
            e0 = t0 - 1
            fw.dma("sp", lambda e: e.dma_start(out=xbuf[:, :, lo:hi], in_=fm(src)[:, :, e0 + lo:e0 + hi]), reads=r_src, writes=r_x)
            norm_mod(l, s_, lo, hi, A2s[l], 3, m)
            if lo == 1:
                fw.op("dve", lambda e: e.memset(hbuf[:, :, 0:1], 0.0), writes=r_h)
            if hi == E - 1:
                fw.op("dve", lambda e: e.memset(hbuf[:, :, E - 1:E], 0.0), writes=r_h)

        def ffn_body1(s_, l, src, dst, r_src, r_dst, t0, T, Lseq, m, final):
            xbuf, hbuf, r_x, r_h = xbufs[s_], hbufs[s_], r_xs[s_], r_hs[s_]
            E = T + 2
            for j in range(NJ):
                wt, rw = nwr()
                wload(wt, rw, wup_d[l, j], wup_b[l, j], r_wup[l][j])
                bg, rbg = nbank()
                mmgroup(bg[:, 0:E], [(wt[:, k, 0:128], hbuf[:, k, 0:E]) for k in range(16)], reads=r_h + [rw], writes=[rbg])
                bv, rbv = nbank()
                mmgroup(bv[:, 0:E], [(wt[:, k, 128:256], hbuf[:, k, 0:E]) for k in range(16)], reads=r_h + [rw], writes=[rbv])
                step_hook()
                outs = []
                for (bk, rb, jj) in ((bg, rbg, j), (bv, rbv, NJ + j)):
                    cw = lambda tap, jj=jj: convw[:, jj * 4 + tap:jj * 4 + tap + 1]
                    ta, rta = ntf()
                    fw.op("act", lambda e, bk=bk, ta=ta, cw=cw: e.activation(out=ta[:, 0:T], in_=bk[:, 1:T + 1], func=AF.Identity,
                                                                             bias=cw(3), scale=cw(1)),
                          reads=[rb, r_cw], writes=[rta])
                    fw.op("dve", lambda e, bk=bk, ta=ta, cw=cw: e.scalar_tensor_tensor(out=ta[:, 0:T], in0=bk[:, 0:T], scalar=cw(0),
                                                                                       in1=ta[:, 0:T], op0=ALU.mult, op1=ALU.add),
                          reads=[rb, r_cw, rta], writes=[rta])
                    fw.op("dve", lambda e, bk=bk, ta=ta, cw=cw: e.scalar_tensor_tensor(out=ta[:, 0:T], in0=bk[:, 2:T + 2], scalar=cw(2),
                                                                                       in1=ta[:, 0:T], op0=ALU.mult, op1=ALU.add),
                          reads=[rb, r_cw, rta], writes=[rta])
                    outs.append((ta, rta))
                (tg, rtg), (tv, rtv) = outs
                sg, rsg_ = ntf()
                fw.op("act", lambda e, tg=tg, sg=sg: e.activation(out=sg[:, 0:T], in_=tg[:, 0:T], func=AF.Silu), reads=[rtg], writes=[rsg_])
                fw.op("dve", lambda e, sg=sg, tv=tv, j=j: e.tensor_tensor(out=actT[:, j, 0:T], in0=sg[:, 0:T], in1=tv[:, 0:T], op=ALU.mult),
                      reads=[rsg_, rtv], writes=[r_act[j]])
        def ffn_body2(s_, l, src, dst, r_src, r_dst, t0, T, Lseq, m, final):
            xbuf, hbuf, r_x, r_h = xbufs[s_], hbufs[s_], r_xs[s_], r_hs[s_]
            for n in range(16):
                bk, rb = nbank()
                for hf in range(2):
                    wd, rwd = ndn()
                    wload(wd, rwd, wdn_d[l, n, hf], wdn_b[l, n, hf], r_wdn[l][n][hf])
                    mmgroup(bk[:, 0:T], [(wd[:, jj, :], actT[:, hf * 22 + jj, 0:T]) for jj in range(22)],
                            reads=r_act[hf * 22:(hf + 1) * 22] + [rwd], writes=[rb], first=(hf == 0), last=(hf == 1))
                fw.op("dve", lambda e, n=n, bk=bk: e.scalar_tensor_tensor(out=xbuf[:, n, 1:T + 1], in0=bk[:, 0:T], scalar=modcol(l, 5, n, m),
                                                                          in1=xbuf[:, n, 1:T + 1], op0=ALU.mult, op1=ALU.add),
                      reads=[rb, r_mods[l], r_x[n]], writes=[r_x[n]])
            if final:
                for c in range(16):
                    fw.op("act", lambda e, c=c: e.activation(out=hbuf[:, c, 1:T + 1], in_=xbuf[:, c, 1:T + 1], func=AF.Square),
                          reads=[r_x[c]], writes=[r_h[c]])
                bk, rb = nbank()
                mmgroup(bk[:, 0:T], [(ones_bf[:], hbuf[:, c, 1:T + 1]) for c in range(16)], reads=r_h + [r_ones], writes=[rb])
                rs, rrs = rsbuf, r_rsbuf
                rsqrt_(rs[:, 0:T], bk[:, 0:T], 0, [rb], rrs)
                for c in range(16):
                    fw.op("dve", lambda e, c=c: e.scalar_tensor_tensor(out=xbuf[:, c, 1:T + 1], in0=xbuf[:, c, 1:T + 1],
                                                                       scalar=gfs[:, c:c + 1], in1=rs[:, 0:T], op0=ALU.mult, op1=ALU.mult),
                          reads=[r_x[c], rrs, r_g], writes=[r_x[c]])
            fw.dma("sp", lambda e: e.dma_start(out=fm(dst)[:, :, t0:t0 + T], in_=xbuf[:, :, 1:T + 1]), reads=r_x, writes=r_dst)

        fw.op("dve", lambda e: e.tensor_scalar(out=gfs[:], in0=gfs[:], scalar1=SQD, scalar2=None, op0=ALU.mult), reads=[r_g], writes=[r_g])
        r_xT = [R("dxT")]; r_ctxT = [R("dctx")]
        r_xm = [R("dxm")]; r_xa = [R("dxa")]; r_cm = [R("dcm")]; r_ca = [R("dca")]; r_out = [R("dout")]
        lat_src = [(xT_d, r_xT), (xa_d, r_xa)]
        ctx_src = [(ctxT_d, r_ctxT), (ca_d, r_ca)]
        done = False
        def run_phase(tiles, prep, body1, body2):
            if not tiles:
                return
            prep(0, *tiles[0])
            for i, t in enumerate(tiles):
                s_ = i % 2
                body1(s_, *t)
                if i + 1 < len(tiles):
                    prep(1 - s_, *tiles[i + 1])
                body2(s_, *t)

        for l in range(DEPTH):
            last = (l == DEPTH - 1)
            if l == 0:
                hook["gen"] = prologue_mod(0)
            drain_hook()
            layer_prologue(l)
            cs, rcs = ctx_src[l]
            xs, rxs = lat_src[l]
            mtiles = [(l, cs, cm_d, rcs, r_cm, 0, LC, LC, 1, not last)]
            mtiles += [(l, xs, xm_d, rxs, r_xm, i * TM, TM, L, 0, True) for i in range(L // TM)]
            fw.barrier()
            run_phase(mtiles, mixer_prep, mixer_body1, mixer_body2)
            if stop_after == "mix%d" % l:
                done = True
                break
            dst, rdst = (out_d, r_out) if last else (xa_d, r_xa)
            ftiles = [(l, xm_d, dst, r_xm, rdst, t0, T, L, 0, last) for (t0, T) in FFN_TILES]
            if not last:
                ftiles.insert(1, (l, cm_d, ca_d, r_cm, r_ca, 0, LC, LC, 1, False))
            fw.barrier()
            if not last:
                hook["gen"] = prologue_mod(l + 1)
            run_phase(ftiles, ffn_prep, ffn_body1, ffn_body2)
            if stop_after == "ffn%d" % l:
                done = True
                break
        fw.barrier()
        xbuf, r_x = xbuf0, r_xs[0]
        if stop_after is not None and stop_after.startswith("mix"):
            for i in range(4):
                fw.dma("sp", lambda e, i=i: e.dma_start(out=xbuf[:, :, :], in_=fm(xm_d)[:, :, i * 512:(i + 1) * 512]), reads=r_xm, writes=r_x)
                fw.dma("sp", lambda e, i=i: e.dma_start(out=fm(out_d)[:, :, i * 512:(i + 1) * 512], in_=xbuf[:, :, :]), reads=r_x, writes=r_out)
        elif stop_after is not None and stop_after == "ffn0":
            for i in range(4):
                fw.dma("sp", lambda e, i=i: e.dma_start(out=xbuf[:, :, :], in_=fm(xa_d)[:, :, i * 512:(i + 1) * 512]), reads=r_xa, writes=r_x)
                fw.dma("sp", lambda e, i=i: e.dma_start(out=fm(out_d)[:, :, i * 512:(i + 1) * 512], in_=xbuf[:, :, :]), reads=r_x, writes=r_out)
        fw.wait_all_dma("sp")
        fw.replay(block)
    return nc


def _tile_kn(w, nw):
    K, N = w.shape
    return np.ascontiguousarray(w.reshape(K // 128, 128, N // nw, nw).transpose(2, 1, 0, 3))


def _fmvec(v):
    return np.ascontiguousarray(v.reshape(-1, 128).T)


def _const_tables():
    inv = (1.0 / (10000.0 ** (np.arange(0, 64, 2, dtype=np.float32) / np.float32(64)))).astype(np.float32)
    t = np.arange(L)
    row = (t // 64).astype(np.float32)
    col = (t % 64).astype(np.float32)
    ang = np.zeros((128, L), np.float32)
    for d in range(128):
        pos = row if d < 64 else col
        ang[d] = pos * inv[d % 32]
    cosT = np.cos(ang).astype(np.float32)
    sinT = np.sin(ang).astype(np.float32)
    rotT = np.zeros((128, 128), np.float32)
    for dp in range(128):
        if (dp % 64) < 32:
            rotT[dp + 32, dp] = -1.0
        else:
            rotT[dp - 32, dp] = 1.0
    s = np.arange(128)[:, None]
    q = np.arange(128)[None, :]
    mp = (s >= q).astype(np.float32)
    mn = (s <= q).astype(np.float32)
    maskP = np.tile(mp, (1, 4))
    maskN = np.tile(mn, (1, 4))

    def invtab(Ls):
        tt = np.arange(Ls)
        out = np.zeros((4, Ls), np.float32)
        for gi, w in enumerate((2, 4, 8, 16)):
            lo = np.clip(tt - w // 2, 0, Ls)
            hi = np.clip(tt - w // 2 + w, 0, Ls)
            out[gi] = 1.0 / (hi - lo).astype(np.float32)
        return np.ascontiguousarray(np.broadcast_to(out[None], (128, 4, Ls)))
    return dict(cosT=cosT, sinT=sinT, rotT=rotT, maskP=maskP, maskN=maskN, invL=invtab(L), invC=invtab(LC))


def _prep_shared(inp):
    f = lambda a: np.asarray(a, dtype=np.float32)
    sh = {}
    w_ada = f(inp["w_ada"])
    sh["wada"] = np.stack([_tile_kn(w_ada[l], 256) for l in range(DEPTH)])
    b_ada = f(inp["b_ada"])
    sh["bada"] = np.stack([np.repeat(_fmvec(b_ada[l]), 2, axis=1) for l in range(DEPTH)])
    sh["g1"] = np.stack([_fmvec(f(inp["norm1_g"])[l]) for l in range(DEPTH)])
    sh["g2"] = np.stack([_fmvec(f(inp["norm2_g"])[l]) for l in range(DEPTH)])
    sh["gf"] = _fmvec(f(inp["final_norm_g"]))
    w_in = f(inp["w_in"])
    sh["win"] = np.stack([_tile_kn(w_in[l], 256) for l in range(DEPTH)])
    w_out = f(inp["w_out"])
    sh["wout"] = np.stack([_tile_kn(w_out[l], 256) for l in range(DEPTH)])
    w_up = f(inp["w_up"])
    ups = []
    for l in range(DEPTH):
        gt = _tile_kn(w_up[l][:, :DFF], 128)
        vt = _tile_kn(w_up[l][:, DFF:], 128)
        ups.append(np.concatenate([gt, vt], axis=3))
    sh["wup"] = np.stack(ups)
    w_dn = f(inp["w_down"])
    dns = []
    for l in range(DEPTH):
        a = w_dn[l].reshape(2, 22, 128, 16, 128).transpose(3, 0, 2, 1, 4)
        dns.append(np.ascontiguousarray(a))
    sh["wdn"] = np.stack(dns)
    cw = f(inp["conv_w"]); cb = f(inp["conv_b"])
    cws = []
    for l in range(DEPTH):
        a = np.concatenate([cw[l], cb[l][None]], axis=0)
        a = a.reshape(4, 88, 128).transpose(2, 1, 0).reshape(128, 352)
        cws.append(np.ascontiguousarray(a))
    sh["convw"] = np.stack(cws)
    sh["qkg"] = np.stack([np.stack([f(inp["q_norm_g"])[l], f(inp["k_norm_g"])[l]], axis=1) for l in range(DEPTH)])
    sh["sink"] = np.stack([np.ascontiguousarray(np.broadcast_to(f(inp["attn_sink"])[l][None], (128, 8))) for l in range(DEPTH)])
    sh["poolw"] = np.stack([np.ascontiguousarray(f(inp["pool_w"])[l].transpose(1, 0, 2)) for l in range(DEPTH)])
    sh["poolsc"] = np.stack([_fmvec(f(inp["pool_scale"])[l]) for l in range(DEPTH)])
    sh["sgug"] = np.stack([np.ascontiguousarray(np.broadcast_to(f(inp["sgu_norm_g"])[l][None], (128, 512))) for l in range(DEPTH)])
    sh["sguw"] = np.stack([np.ascontiguousarray(f(inp["sgu_w"])[l].transpose(2, 0, 1)) for l in range(DEPTH)])
    sh["sgub"] = np.stack([np.ascontiguousarray(np.broadcast_to(f(inp["sgu_b"])[l][None], (128, 4, 128))) for l in range(DEPTH)])
    sh.update(_const_tables())
    return sh


_NC_CACHE = {}


def kernel(x, c, ctx, c_ctx, norm1_g, norm2_g, w_ada, b_ada, w_in, q_norm_g, k_norm_g,
           attn_sink, pool_w, pool_scale, sgu_norm_g, sgu_w, sgu_b, w_out, w_up, conv_w,
           conv_b, w_down, final_norm_g, _stop_after=None, _cores=None, _trace=False):
    inp = dict(w_ada=w_ada, b_ada=b_ada, norm1_g=norm1_g, norm2_g=norm2_g, final_norm_g=final_norm_g,
               w_in=w_in, w_out=w_out, w_up=w_up, w_down=w_down, conv_w=conv_w, conv_b=conv_b,
               q_norm_g=q_norm_g, k_norm_g=k_norm_g, attn_sink=attn_sink, pool_w=pool_w,
               pool_scale=pool_scale, sgu_norm_g=sgu_norm_g, sgu_w=sgu_w, sgu_b=sgu_b)
    sh = _prep_shared(inp)
    x = np.asarray(x, np.float32); ctx = np.asarray(ctx, np.float32)
    c = np.asarray(c, np.float32); c_ctx = np.asarray(c_ctx, np.float32)
    cores = list(range(x.shape[0])) if _cores is None else _cores
    in_maps = []
    for b in cores:
        mcore = dict(sh)
        mcore["xT"] = np.ascontiguousarray(x[b].T)
        mcore["ctxT"] = np.ascontiguousarray(ctx[b].T)
        cc = np.stack([_fmvec(c[b]), _fmvec(c_ctx)], axis=2).reshape(128, 32)
        mcore["cc"] = np.ascontiguousarray(cc)
        in_maps.append(mcore)
    key = _stop_after
    if key not in _NC_CACHE:
        _NC_CACHE[key] = build_program(_stop_after)
    nc = _NC_CACHE[key]
    if _trace:
        res = run_bass_kernel_spmd(nc, in_maps, core_ids=list(range(len(cores))), trace=True)
        print("exec_time_ns", res.exec_time_ns)
    else:
        res = run_bass_kernel_spmd(nc, in_maps, core_ids=list(range(len(cores))))
    out = np.stack([np.ascontiguousarray(r["outT"].T) for r in res.results], axis=0)
    return out.astype(np.float32)
```
